# Optimizing a Trainium2 kernel written in Bass

```python
import jax, jax.numpy as jnp
from jax import lax
import numpy as np

D_MODEL = 2048
BATCH = 2
SEQ = 16384
DEPTH = 1

GRID_W = 64
WIN_R = 8
WIN_C = 16
ATTN_HEADS = 8
HEAD_DIM = 128
D_ATTN = ATTN_HEADS * HEAD_DIM
D_CONV = D_MODEL // 2
CONV_WIDTH = 31
CONV_PAD = CONV_WIDTH // 2
D_FF = -(-8 * D_MODEL // (3 * 256)) * 256
EPS = 1e-6
IN_SPLITS = [D_ATTN, D_ATTN, D_ATTN, D_CONV, D_CONV, D_MODEL, D_MODEL]
N_IN = sum(IN_SPLITS)

kernel_name = "hybrid_natten_conformer_swiglu"


def rms_norm(x, g):
    xf = x.astype(jnp.float32)
    y = xf * lax.rsqrt(jnp.mean(xf * xf, axis=-1, keepdims=True) + EPS) * g.astype(jnp.float32)
    return y.astype(x.dtype)


def layer_norm(x, g, b):
    xf = x.astype(jnp.float32)
    mu = jnp.mean(xf, axis=-1, keepdims=True)
    xc = xf - mu
    var = jnp.mean(xc * xc, axis=-1, keepdims=True)
    y = xc * lax.rsqrt(var + EPS) * g.astype(jnp.float32) + b.astype(jnp.float32)
    return y.astype(x.dtype)


def neighbourhood_attention(q, k, v, rpb):
    B, S, H, Dh = q.shape
    rows = S // GRID_W
    kr = min(WIN_R, rows)
    kc = min(WIN_C, GRID_W)
    qg = q.reshape(B, rows, GRID_W, H, Dh)
    kg = k.reshape(B, rows, GRID_W, H, Dh)
    vg = v.reshape(B, rows, GRID_W, H, Dh)
    col = jnp.arange(GRID_W)
    col_start = jnp.clip(col - kc // 2, 0, GRID_W - kc)
    col_idx = col_start[:, None] + jnp.arange(kc)[None, :]
    dc = col_idx - col[:, None]
    rpb_cols = rpb[:, :, dc + WIN_C - 1]
    scale = HEAD_DIM ** -0.5

    def one_row(r):
        r_start = jnp.clip(r - kr // 2, 0, rows - kr)
        kb = lax.dynamic_slice_in_dim(kg, r_start, kr, axis=1)
        vb = lax.dynamic_slice_in_dim(vg, r_start, kr, axis=1)
        kw = kb[:, :, col_idx]
        vw = vb[:, :, col_idx]
        qr = lax.dynamic_index_in_dim(qg, r, axis=1, keepdims=False)
        s = jnp.einsum('bwhd,biwjhd->bhwij', qr, kw).astype(jnp.float32) * scale
        dr = r_start + jnp.arange(kr) - r
        bias = jnp.transpose(rpb_cols[:, dr + WIN_R - 1], (0, 2, 1, 3))
        s = s + bias[None].astype(jnp.float32)
        p = jax.nn.softmax(s.reshape(B, H, GRID_W, kr * kc), axis=-1)
        p = p.reshape(B, H, GRID_W, kr, kc).astype(vw.dtype)
        return jnp.einsum('bhwij,biwjhd->bwhd', p, vw)

    out = lax.map(one_row, jnp.arange(rows))
    return jnp.transpose(out, (1, 0, 2, 3, 4)).reshape(B, S, H * Dh)


def conformer_conv(a, b, w_dw, b_dw, ln_g, ln_b, w_pw):
    z = a * jax.nn.sigmoid(b)
    z = lax.conv_general_dilated(
        z, w_dw[:, None, :].astype(z.dtype), window_strides=(1,),
        padding=[(CONV_PAD, CONV_PAD)],
        dimension_numbers=('NWC', 'WIO', 'NWC'),
        feature_group_count=z.shape[-1]) + b_dw
    z = layer_norm(z, ln_g, ln_b)
    z = jax.nn.silu(z)
    return z @ w_pw


def setup_inputs(seed: int = 0) -> dict:
    key = jax.random.key(seed)
    ks = jax.random.split(key, 20)
    f32 = jnp.float32
    nrm = lambda k, shape, s: jax.random.normal(k, shape, f32) * s
    gain = lambda k, n: 1.0 + 0.05 * jax.random.normal(k, (DEPTH, n), f32)
    return {
        'x': jax.random.normal(ks[0], (BATCH, SEQ, D_MODEL), f32),
        'mix_pre_g': gain(ks[1], D_MODEL),
        'mix_post_g': gain(ks[2], D_MODEL),
        'w_in': nrm(ks[3], (DEPTH, D_MODEL, N_IN), D_MODEL ** -0.5),
        'rpb': nrm(ks[4], (DEPTH, ATTN_HEADS, 2 * WIN_R - 1, 2 * WIN_C - 1), 0.1),
        'w_attn_o': nrm(ks[5], (DEPTH, D_ATTN, D_MODEL), D_ATTN ** -0.5),
        'w_dw': nrm(ks[6], (DEPTH, CONV_WIDTH, D_CONV), CONV_WIDTH ** -0.5),
        'b_dw': nrm(ks[7], (DEPTH, D_CONV), 0.02),
        'conv_ln_g': gain(ks[8], D_CONV),
        'conv_ln_b': nrm(ks[9], (DEPTH, D_CONV), 0.02),
        'w_conv_o': nrm(ks[10], (DEPTH, D_CONV, D_MODEL), D_CONV ** -0.5),
        'w_out': nrm(ks[11], (DEPTH, D_MODEL, D_MODEL), D_MODEL ** -0.5),
        'ffn_pre_g': gain(ks[12], D_MODEL),
        'ffn_post_g': gain(ks[13], D_MODEL),
        'w_gate_up': nrm(ks[14], (DEPTH, D_MODEL, 2 * D_FF), D_MODEL ** -0.5),
        'w_down': nrm(ks[15], (DEPTH, D_FF, D_MODEL), D_FF ** -0.5),
    }


def reference(x, mix_pre_g, mix_post_g, w_in, rpb, w_attn_o, w_dw, b_dw,
              conv_ln_g, conv_ln_b, w_conv_o, w_out, ffn_pre_g, ffn_post_g,
              w_gate_up, w_down):
    B, S, D = x.shape
    split_pts = list(np.cumsum(IN_SPLITS)[:-1])
    h = x
    for l in range(DEPTH):
        u = rms_norm(h, mix_pre_g[l])
        proj = u @ w_in[l]
        q, k, v, ca, cb, ga, gb = jnp.split(proj, split_pts, axis=-1)
        q = q.reshape(B, S, ATTN_HEADS, HEAD_DIM)
        k = k.reshape(B, S, ATTN_HEADS, HEAD_DIM)
        v = v.reshape(B, S, ATTN_HEADS, HEAD_DIM)
        y_attn = neighbourhood_attention(q, k, v, rpb[l]) @ w_attn_o[l]
        y_conv = conformer_conv(ca, cb, w_dw[l], b_dw[l], conv_ln_g[l],
                                conv_ln_b[l], w_conv_o[l])
        merged = jax.nn.sigmoid(ga) * y_attn + jax.nn.sigmoid(gb) * y_conv
        h = h + rms_norm(merged @ w_out[l], mix_post_g[l])
        u = rms_norm(h, ffn_pre_g[l])
        gate, up = jnp.split(u @ w_gate_up[l], 2, axis=-1)
        f = (jax.nn.silu(gate) * up) @ w_down[l]
        h = h + rms_norm(f, ffn_post_g[l])
    return h
```

```python
import numpy as np
import concourse.bass as bass
import concourse.mybir as mybir
from concourse.bass_utils import run_bass_kernel_spmd

F32 = mybir.dt.float32
BF16 = mybir.dt.bfloat16
AF = mybir.ActivationFunctionType
ALU = mybir.AluOpType
AX = mybir.AxisListType

D = 2048
NCORES = 8
TOK_CORE = 4096
NT_EXT = 36
NGRP = 9
DFF = 5632
NJ = 44
EPS = 1e-6
QSCALE = 128 ** -0.5
S_A, S_V, S_C, S_O, S_GU, S_D = 16, 4, 24, 8, 44, 22
OFF_A = 0
OFF_V = OFF_A + S_A
OFF_C = OFF_V + S_V
OFF_O = OFF_C + S_C
OFF_GU = OFF_O + S_O
OFF_D = OFF_GU + S_GU
NSLAB = OFF_D + S_D
NS1 = OFF_GU
NCV = 16 + 16 + 8 * 31 + 24
CV_GPRE1, CV_GPRE2, CV_WDW, CV_BDW, CV_LNG, CV_LNB = 0, 16, 32, 32 + 248, 32 + 256, 32 + 264
NSB = 3


class Tile:
    def __init__(self, name, exclusive=False):
        self.name = name
        self.exclusive = exclusive
        self.writer = None
        self.readers = []
        self.aliases = []

    def group(self):
        return [self] + self.aliases


def alias(a, b):
    a.aliases.append(b)
    b.aliases.append(a)


class DmaSem:
    def __init__(self, sem):
        self.sem = sem
        self.count = 0


class Op:
    __slots__ = ("eng", "fn", "deps", "dma", "semval", "needs_inc", "idx", "late")

    def __init__(self, eng, fn, dma):
        self.eng = eng
        self.fn = fn
        self.dma = dma
        self.deps = []
        self.semval = None
        self.needs_inc = False
        self.idx = -1
        self.late = False


ENGS = ["pe", "act", "dve", "pool", "sp"]


class Prog:
    def __init__(self, nc, esems):
        self.nc = nc
        self.ops = {e: [] for e in ENGS}
        self.esem = esems
        self.dmasems = []

    def new_dma_sem(self, sem):
        d = DmaSem(sem)
        self.dmasems.append(d)
        return d

    def op(self, eng, fn, reads=(), writes=(), dma=None, late=False):
        o = Op(eng, fn, dma)
        o.late = late
        writes = list(writes) + [t for t in reads if t.exclusive]
        reads = [t for t in reads if not t.exclusive]
        raw = []
        for t in reads:
            for tt in t.group():
                if tt.writer is not None:
                    raw.append(tt.writer)
        for t in writes:
            for tt in t.group():
                if tt.writer is not None:
                    raw.append(tt.writer)
                raw.extend(tt.readers)
        best = {}
        deps = []
        seen = set()
        for d in raw:
            if id(d) in seen:
                continue
            seen.add(id(d))
            if d.dma is not None:
                deps.append(d)
            else:
                if d.eng == eng and dma is None and not d.late:
                    continue
                b = best.get(d.eng)
                if b is None or d.idx > b.idx:
                    best[d.eng] = d
        deps.extend(best.values())
        o.deps = deps
        for d in deps:
            if d.dma is None:
                d.needs_inc = True
        for t in reads:
            t.readers.append(o)
        for t in writes:
            t.writer = o
            t.readers = []
        if dma is not None:
            dma.count += 16
            o.semval = dma.count
        o.idx = len(self.ops[eng])
        self.ops[eng].append(o)
        return o

    def emit(self):
        nc = self.nc
        for e in ENGS:
            c = 0
            for o in self.ops[e]:
                if o.dma is None and o.needs_inc:
                    c += 1
                    o.semval = c
            assert c < 60000, (e, c)
        for d in self.dmasems:
            assert d.count < 60000

        def run(ename, eng):
            known = {}
            for o in self.ops[ename]:
                need = {}
                for d in o.deps:
                    s = d.dma.sem if d.dma is not None else self.esem[d.eng]
                    k = id(s)
                    if k not in need or need[k][1] < d.semval:
                        need[k] = (s, d.semval)
                for k, (s, v) in need.items():
                    if known.get(k, 0) >= v:
                        continue
                    eng.wait_ge(s, v)
                    known[k] = v
                ins = o.fn(eng)
                if o.dma is not None:
                    ins.then_inc(o.dma.sem, 16)
                elif o.needs_inc:
                    ins.then_inc(self.esem[ename], 1)
            if ename == "sp":
                for d in self.dmasems:
                    if d.count > 0:
                        eng.wait_ge(d.sem, d.count)

        with nc.Block() as block:
            @block.tensor
            def _(e):
                run("pe", e)

            @block.scalar
            def _(e):
                run("act", e)

            @block.vector
            def _(e):
                run("dve", e)

            @block.gpsimd
            def _(e):
                run("pool", e)

            @block.sync
            def _(e):
                run("sp", e)


class SlabStream:
    def __init__(self, P, wbf, bufs, buf_tiles, dsems, src_tile):
        self.P = P
        self.wbf = wbf
        self.bufs = bufs
        self.tiles = buf_tiles
        self.dsems = dsems
        self.src_tile = src_tile
        self.loaded = {}
        self.counter = 0
        self.resident = [None] * len(bufs)

    def get(self, slab):
        if slab in self.loaded:
            return self.loaded[slab]
        i = self.counter % len(self.bufs)
        self.counter += 1
        old = self.resident[i]
        if old is not None:
            del self.loaded[old]
        self.resident[i] = slab
        self.loaded[slab] = i
        buf = self.bufs[i]
        src = self.wbf[slab * 128:(slab + 1) * 128, :]
        self.P.op("sp", lambda e, buf=buf, src=src: e.dma_start(out=buf[:, :], in_=src),
                  reads=([self.src_tile[slab]] if slab in self.src_tile else []), writes=[self.tiles[i]], dma=self.dsems[i])
        return i

    def lhs_block(self, base, b):
        i = self.get(base + b // 32)
        o = (b % 32) * 128
        return self.bufs[i][:, o:o + 128], self.tiles[i]

    def rhs_block(self, base, b):
        i = self.get(base + b // 8)
        o = (b % 8) * 512
        return self.bufs[i][:, o:o + 512], self.tiles[i]


def rstd_chain(P, src, dst, tmp, tiles, scale):
    P.op("dve", lambda e: e.tensor_scalar(out=dst, in0=src, scalar1=scale, scalar2=EPS, op0=ALU.mult, op1=ALU.add),
         reads=tiles, writes=tiles)
    P.op("act", lambda e: e.activation(out=tmp, in_=dst, func=AF.Sqrt), reads=tiles, writes=tiles)
    P.op("dve", lambda e: e.reciprocal(out=dst, in_=tmp), reads=tiles, writes=tiles, late=True)

def build_program(dbg=None):
    nc = bass.Bass("TRN2", target_bir_lowering=False)
    x_ext = nc.dram_tensor("x_ext", [NT_EXT, 128, D], F32, kind="ExternalInput").ap()
    wall = nc.dram_tensor("wall", [NSLAB * 128, 4096], F32, kind="ExternalInput").ap()
    cvec_d = nc.dram_tensor("cvec", [128, NCV], F32, kind="ExternalInput").ap()
    gpost_d = nc.dram_tensor("gpost", [2, 128, D], F32, kind="ExternalInput").ap()
    tabs_d = nc.dram_tensor("tabs", [5, 128, 6144], F32, kind="ExternalInput").ap()
    ident_d = nc.dram_tensor("ident", [128, 128], F32, kind="ExternalInput").ap()
    out_d = nc.dram_tensor("out", [32, 128, D], F32, kind="ExternalOutput").ap()
    wbf = nc.dram_tensor("wbf", [(NSLAB + 4) * 128, 4096], BF16, kind="Internal").ap()
    ebt = nc.dram_tensor("ebt", [5, 128, 6144], BF16, kind="Internal").ap()
    hbuf = nc.dram_tensor("hbuf", [32, 128, D], F32, kind="Internal").ap()
    dbg_d = {}
    if dbg:
        for name, shape, dt in dbg:
            dbg_d[name] = nc.dram_tensor(name, shape, dt, kind="ExternalOutput").ap()

    from contextlib import ExitStack

    with ExitStack() as es:
        def sb(name, shape, dt):
            return es.enter_context(nc.sbuf_tensor(name, shape, dt))

        def sem(name):
            return es.enter_context(nc.semaphore(name))

        esems = {e: sem("e_" + e) for e in ENGS}
        P = Prog(nc, esems)

        psall = es.enter_context(nc.psum_tensor("psall", [128, 4096], F32))
        psall_bf = psall.bitcast(BF16)
        PSB = [Tile("psb%d" % i, exclusive=True) for i in range(8)]

        def psf(b, n=512, nb=1):
            return psall[:, b * 512:b * 512 + n]

        ident = sb("ident_sb", [128, 128], BF16)
        ones_f = sb("ones_f", [128, 128], F32)
        ones_b = sb("ones_b", [128, 128], BF16)
        cvec = sb("cvec_sb", [128, NCV], F32)
        gpost = sb("gpost_sb", [128, D], F32)
        uT = sb("uT", [128, 16, 768], BF16)
        qT = sb("qT", [128, 8, 1024], BF16)
        kT = sb("kT", [128, 8, 1024], BF16)
        vv = sb("vv", [128, 8, 1024], BF16)
        zT = sb("zT", [128, 8, 1024], BF16)
        slabs = [sb("slab%d" % i, [128, 4096], BF16) for i in range(NSB)]
        xs = [sb("xs%d" % i, [128, D], F32) for i in range(2)]
        utms = [sb("utm%d" % i, [128, D], BF16) for i in range(2)]
        utm = utms[0]
        ar1 = sb("ar1", [128, 8192], F32)
        ar2 = sb("ar2", [128, 8192], BF16)
        sigt = [sb("sigt%d" % i, [128, 512], F32) for i in range(2)]
        tmpf = [sb("tmpf%d" % i, [128, 512], F32) for i in range(2)]
        stat = sb("stat", [128, 64], F32)
        sjunk = sb("sjunk", [128, 8], F32)
        rrs = [sb("rr%d" % i, [128, 128], F32) for i in range(2)]
        sqj1 = sb("sqj1", [128, 512], BF16)

        ycv = ar1[:, 0:4096].rearrange("p (c t) -> p c t", c=8)
        ar1_bf = ar1.bitcast(BF16)
        attnT = ar1_bf[:, 8192:12288].rearrange("p (c t) -> p c t", c=8)
        cT = ar1_bf[:, 12288:16384].rearrange("p (c t) -> p c t", c=8)
        sga = ar1_bf[:, 0:8192].rearrange("p (c t) -> p c t", c=16)
        o1 = ar1[:, :].rearrange("p (t d) -> p t d", t=4)
        tb = ar2[:, 0:6144].rearrange("p (h s q) -> p h s q", h=8, s=6)
        ets = [ar2[:, 6144 + i * 768:6144 + (i + 1) * 768] for i in range(2)]

        def mT(m, par):
            return qT[:, m, par * 512:(par + 1) * 512] if m < 8 else kT[:, m - 8, par * 512:(par + 1) * 512]

        T_const = Tile("const")
        T_uT = Tile("uT")
        T_q = [Tile("q%d" % i) for i in range(2)]
        T_k = [Tile("k%d" % i) for i in range(2)]
        T_v = [Tile("v%d" % i) for i in range(8)]
        T_z = Tile("z")
        T_slab = [Tile("slab%d" % i) for i in range(NSB)]
        T_xs = [Tile("xs%d" % i) for i in range(2)]
        T_utms = [Tile("utm%d" % i) for i in range(2)]
        T_utm = T_utms[0]
        T_ycv = [Tile("ycv%d" % i) for i in range(8)]
        T_attn = Tile("attnT")
        T_cT = Tile("cT")
        T_o1 = [Tile("o1_%d" % i) for i in range(4)]
        for t in T_o1:
            for u in T_ycv + [T_attn, T_cT]:
                alias(t, u)
        T_tb = Tile("tb")
        T_et = [Tile("et%d" % i) for i in range(2)]
        T_mTp = [Tile("mergedT%d" % i) for i in range(2)]
        for i in range(2):
            alias(T_mTp[i], T_q[i])
            alias(T_mTp[i], T_k[i])
        T_sig = [Tile("sig%d" % i) for i in range(2)]
        T_tmp = [Tile("tmp%d" % i) for i in range(2)]
        T_stat = [Tile("stat%d" % i) for i in range(8)]
        T_rr = [Tile("rr%d" % i) for i in range(2)]
        T_dg = [Tile("dg%d" % i) for i in range(4)]
        T_sqj1 = Tile("sqj1")
        T_w1 = Tile("wbf_s1")
        T_w2 = Tile("wbf_s2")
        T_ebt = Tile("ebt")
        T_hbuf = [Tile("hbuf%d" % i) for i in range(32)]
        T_dbg = Tile("dbg")

        D_slab = [P.new_dma_sem(sem("d_slab%d" % i)) for i in range(NSB)]
        D_xs = [P.new_dma_sem(sem("d_xs%d" % i)) for i in range(2)]
        D_c1 = [P.new_dma_sem(sem("d_cast%d" % i)) for i in range(4)]
        D_consts = [P.new_dma_sem(sem("d_const%d" % i)) for i in range(3)]
        D_tb = P.new_dma_sem(sem("d_tb"))
        D_dgw = P.new_dma_sem(sem("d_dgw"))
        D_ebt = P.new_dma_sem(sem("d_ebt"))
        D_h = [P.new_dma_sem(sem("d_h%d" % i)) for i in range(4)]
        D_acc = [P.new_dma_sem(sem("d_acc%d" % i)) for i in range(4)]
        early_loads = {}
        D_dbg = P.new_dma_sem(sem("d_dbg"))

        slab_tile = {}

        def cast(bounds, dsem):
            for k in range(len(bounds) - 1):
                s_, e_ = bounds[k], bounds[k + 1]
                tt = Tile("cast%d" % s_)
                P.op("pool", lambda e, s_=s_, e_=e_: e.dma_start(out=wbf[s_ * 128:e_ * 128, :], in_=wall[s_ * 128:e_ * 128, :]),
                     writes=[tt], dma=dsem[k])
                for j in range(s_, e_):
                    slab_tile[j] = tt
        cast([0, 8, 20, 36, NS1], D_c1)
        P.op("pool", lambda e: e.dma_start(out=ident[:, :], in_=ident_d[:, :]), writes=[T_const], dma=D_consts[0])
        T_cv = Tile("cvec")
        P.op("pool", lambda e: e.dma_start(out=cvec[:, :], in_=cvec_d[:, :]), writes=[T_cv], dma=D_consts[1])
        T_gp = Tile("gpost")
        P.op("pool", lambda e: e.dma_start(out=gpost[:, :], in_=gpost_d[0]), writes=[T_gp], dma=D_consts[2])
        T_ones = Tile("ones")
        P.op("dve", lambda e: e.memset(ones_f[:, :], 1.0), writes=[T_ones])
        P.op("dve", lambda e: e.memset(ones_b[:, :], 1.0), writes=[T_ones])
        P.op("pool", lambda e: e.memset(zT[:, :, :], 0.0), writes=[T_z])
        D_c2 = [P.new_dma_sem(sem("d_cast2_%d" % i)) for i in range(5)]
        c2_bounds = [NS1, NS1 + 13, NS1 + 26, NS1 + 39, NS1 + 52, NSLAB]

        def cast2(k):
            s_, e_ = c2_bounds[k], c2_bounds[k + 1]
            P.op("pool", lambda e: e.dma_start(out=wbf[s_ * 128:e_ * 128, :], in_=wall[s_ * 128:e_ * 128, :]),
                 writes=[Tile("cast2_%d" % k)], dma=D_c2[k])
        SS = SlabStream(P, wbf, slabs, T_slab, D_slab, slab_tile)

        cnt = {"xs": 0, "pt": 0, "ps": 0, "sig": 0, "stat": 0, "pv": 0, "ev": 0, "ctmp": 0, "dg": 0}

        def load_x(src_ap):
            i = cnt["xs"] % 2
            cnt["xs"] += 1
            P.op("act", lambda e: e.dma_start(out=xs[i][:, :], in_=src_ap), writes=[T_xs[i]], dma=D_xs[i])
            return i

        def pre_x(i):
            ui = cnt["stat"] % 2
            utm = utms[ui]
            T_utm = T_utms[ui]
            si = cnt["stat"] % 8
            cnt["stat"] += 1
            sq = stat[:, si * 8:si * 8 + 1]
            rs_ = stat[:, si * 8 + 1:si * 8 + 2]
            P.op("act", lambda e: e.activation(out=utm[:, :], in_=xs[i][:, :], func=AF.Square, accum_out=sq),
                 reads=[T_xs[i]], writes=[T_utm, T_stat[si]])
            rstd_chain(P, sq, rs_, stat[:, si * 8 + 2:si * 8 + 3], [T_stat[si]], 1.0 / D)
            P.op("dve", lambda e: e.tensor_scalar(out=utm[:, :], in0=xs[i][:, :], scalar1=rs_, scalar2=None, op0=ALU.mult),
                 reads=[T_xs[i], T_stat[si]], writes=[T_utm])
            return ui

        def post_x(ui, col, gcol0):
            utm = utms[ui]
            T_utm = T_utms[ui]
            for half in range(2):
                b = 6 + half
                for j in range(8):
                    kc = half * 8 + j
                    P.op("pe", lambda e, b=b, j=j, kc=kc: e.transpose(
                        out=psall_bf[:, b * 1024 + j * 128:b * 1024 + (j + 1) * 128],
                        in_=utm[:, kc * 128:(kc + 1) * 128], identity=ident[:, :]),
                        reads=[T_utm, T_const], writes=[PSB[b]])
                for j in range(8):
                    kc = half * 8 + j
                    src = psall_bf[:, b * 1024 + j * 128:b * 1024 + (j + 1) * 128]
                    dst = uT[:, kc, col:col + 128]
                    gs = cvec[:, gcol0 + kc:gcol0 + kc + 1]
                    if half == 0:
                        P.op("act", lambda e, src=src, dst=dst, gs=gs: e.activation(out=dst, in_=src, func=AF.Copy, scale=gs),
                             reads=[PSB[b], T_cv], writes=[T_uT])
                    else:
                        P.op("dve", lambda e, src=src, dst=dst, gs=gs: e.tensor_scalar(out=dst, in0=src, scalar1=gs, scalar2=None, op0=ALU.mult),
                             reads=[PSB[b], T_cv], writes=[T_uT])

        staged = {}

        def stage_x(g, tiles):
            for t in tiles:
                i = load_x(x_ext[4 * g + t])
                staged[(g, t)] = pre_x(i)

        def next_ps():
            b = cnt["ps"] % 4
            cnt["ps"] += 1
            return b

        def proj_lhs(base, b0, nk, rhs_fn, rhs_tiles, n=512):
            b = next_ps()
            for kc in range(nk):
                w, wt = SS.lhs_block(base, b0 + kc)
                rhs = rhs_fn(kc)
                P.op("pe", lambda e, b=b, w=w, rhs=rhs, kc=kc: e.matmul(psf(b, n), lhsT=w, rhs=rhs, start=(kc == 0), stop=(kc == nk - 1)),
                     reads=[wt] + rhs_tiles, writes=[PSB[b]])
            return b

        def dump(name, sb_ap, tiles):
            if name in dbg_d:
                P.op("sp", lambda e: e.dma_start(out=dbg_d[name], in_=sb_ap), reads=tiles, writes=[T_dbg], dma=D_dbg)

        DVE_CTS = [0, 1, 2, 3]
        PE_CTS = [4, 5, 6, 7]

        def conv_dve(ct):
            y = ycv[:, ct, :]
            P.op("dve", lambda e: e.tensor_scalar(out=y, in0=zT[:, ct, 241:241 + 512],
                                                  scalar1=cvec[:, CV_WDW + ct * 31:CV_WDW + ct * 31 + 1],
                                                  scalar2=cvec[:, CV_BDW + ct:CV_BDW + ct + 1], op0=ALU.mult, op1=ALU.add),
                 reads=[T_z, T_cv], writes=[T_ycv[ct]])
            for k in range(1, 31):
                wk = cvec[:, CV_WDW + ct * 31 + k:CV_WDW + ct * 31 + k + 1]
                zin = zT[:, ct, 241 + k:241 + k + 512]
                P.op("dve", lambda e, wk=wk, zin=zin: e.scalar_tensor_tensor(out=y, in0=zin, scalar=wk, in1=y, op0=ALU.mult, op1=ALU.add),
                     reads=[T_z, T_cv], writes=[T_ycv[ct]])

        def conv_pe(ct):
            b = next_ps()
            ci = PE_CTS.index(ct)
            for k in range(31):
                zin = zT[:, ct, 241 + k:241 + k + 512]
                w, wt = SS.lhs_block(NSLAB, ci * 31 + k)
                P.op("pe", lambda e, b=b, k=k, zin=zin, w=w: e.matmul(psf(b), lhsT=w, rhs=zin, start=(k == 0), stop=(k == 30)),
                     reads=[wt, T_z], writes=[PSB[b]])
            P.op("act", lambda e, b=b: e.activation(out=ycv[:, ct, :], in_=psf(b), func=AF.Identity, bias=cvec[:, CV_BDW + ct:CV_BDW + ct + 1]),
                 reads=[PSB[b], T_cv], writes=[T_ycv[ct]])

        def diag_prologue():
            T_dgw = Tile("dgw")
            nblk = len(PE_CTS) * 31
            for sl in range(4):
                buf = slabs[sl % NSB]
                tl = T_slab[sl % NSB]
                for j in range(32):
                    blk = sl * 32 + j
                    dst = buf[:, j * 128:(j + 1) * 128]
                    if blk < nblk:
                        ct = PE_CTS[blk // 31]
                        k = blk % 31
                        wk = cvec[:, CV_WDW + ct * 31 + k:CV_WDW + ct * 31 + k + 1]
                        if j % 2 == 0:
                            P.op("act", lambda e, dst=dst, wk=wk: e.activation(out=dst, in_=ident[:, :], func=AF.Copy, scale=wk),
                                 reads=[T_const, T_cv], writes=[tl])
                        else:
                            P.op("dve", lambda e, dst=dst, wk=wk: e.tensor_scalar(out=dst, in0=ident[:, :], scalar1=wk, scalar2=None, op0=ALU.mult),
                                 reads=[T_const, T_cv], writes=[tl])
                    else:
                        P.op("dve", lambda e, dst=dst: e.memset(dst, 0.0), writes=[tl])
                P.op("sp", lambda e, sl=sl, buf=buf: e.dma_start(out=wbf[(NSLAB + sl) * 128:(NSLAB + sl + 1) * 128, :], in_=buf[:, :]),
                     reads=[tl], writes=[T_dgw], dma=D_dgw)
            for sl in range(4):
                slab_tile[NSLAB + sl] = T_dgw

        def phase_A(g):
            rc = (g % 2) * 512
            if g == 1:
                P.op("pool", lambda e: e.tensor_copy(out=zT[:, :, 0:512], in_=zT[:, :, 512:1024]), reads=[T_z], writes=[T_z])
                P.op("pool", lambda e: e.tensor_copy(out=uT[:, :, 0:256], in_=uT[:, :, 512:768]), reads=[T_uT], writes=[T_uT])
            if (g, 0) in staged and (g, 1) in staged:
                post_x(staged.pop((g, 0)), 256, CV_GPRE1)
                i2 = early_loads.pop((g, 2)) if (g, 2) in early_loads else load_x(x_ext[4 * g + 2])
                i3 = early_loads.pop((g, 3)) if (g, 3) in early_loads else load_x(x_ext[4 * g + 3])
                u2 = pre_x(i2)
                post_x(staged.pop((g, 1)), 256 + 128, CV_GPRE1)
                u3 = pre_x(i3)
                post_x(u2, 256 + 256, CV_GPRE1)
                post_x(u3, 256 + 384, CV_GPRE1)
            else:
                for t in range(4):
                    if (g, t) not in staged:
                        staged[(g, t)] = pre_x(load_x(x_ext[4 * g + t]))
                    post_x(staged.pop((g, t)), 256 + t * 128, CV_GPRE1)
            rhs_u = lambda kc: uT[:, kc, 256:768]
            for j in range(8):
                ba = proj_lhs(OFF_A, (2 * j) * 16, 16, rhs_u, [T_uT])
                bb = proj_lhs(OFF_A, (2 * j + 1) * 16, 16, rhs_u, [T_uT])
                si = cnt["sig"] % 2
                cnt["sig"] += 1
                P.op("act", lambda e, bb=bb, si=si: e.activation(out=sigt[si][:, :], in_=psf(bb), func=AF.Sigmoid),
                     reads=[PSB[bb]], writes=[T_sig[si]])
                P.op("dve", lambda e, ba=ba, si=si, j=j: e.tensor_tensor(out=zT[:, j, 512:1024], in0=psf(ba), in1=sigt[si][:, :], op=ALU.mult),
                     reads=[PSB[ba], T_sig[si]], writes=[T_z])
            if g >= 1:
                for ct in DVE_CTS:
                    conv_dve(ct)
                for ct in PE_CTS:
                    conv_pe(ct)
            for m in range(8):
                b = proj_lhs(OFF_A, (16 + m) * 16, 16, rhs_u, [T_uT])
                P.op("act", lambda e, b=b, m=m: e.activation(out=qT[:, m, rc:rc + 512], in_=psf(b), func=AF.Copy, scale=QSCALE),
                     reads=[PSB[b]], writes=[T_q[g % 2]])
            for m in range(8):
                b = proj_lhs(OFF_A, (24 + m) * 16, 16, rhs_u, [T_uT])
                P.op("act", lambda e, b=b, m=m: e.activation(out=kT[:, m, rc:rc + 512], in_=psf(b), func=AF.Copy),
                     reads=[PSB[b]], writes=[T_k[g % 2]])
            for nb in range(2):
                for t in range(4):
                    b = 4 + (cnt["pv"] % 2)
                    cnt["pv"] += 1
                    for kc in range(16):
                        w, wt = SS.rhs_block(OFF_V, nb * 16 + kc)
                        P.op("pe", lambda e, b=b, w=w, kc=kc, t=t: e.matmul(psf(b), lhsT=uT[:, kc, 256 + t * 128:256 + (t + 1) * 128], rhs=w,
                                                                          start=(kc == 0), stop=(kc == 15)),
                             reads=[wt, T_uT], writes=[PSB[b]])
                    slot = (4 * g + t) % 8
                    dst = vv[:, slot, nb * 512:(nb + 1) * 512]
                    P.op("act", lambda e, b=b, dst=dst: e.activation(out=dst, in_=psf(b), func=AF.Copy),
                         reads=[PSB[b]], writes=[T_v[slot]])

        state = {"tbv": None}

        def ring_col(e_):
            return ((e_ // 4) % 2) * 512 + (e_ % 4) * 128

        def phase_B(g):
            e0 = 4 * g - 2
            items = []
            for e_ in range(e0, e0 + 4):
                p = e_ - 2
                var = {0: 0, 1: 1, 30: 3, 31: 4}.get(p, 2)
                offs = {0: [-2, -1, 0, 1, 2, 3], 4: [-3, -2, -1, 0, 1, 2]}.get(var, [-2, -1, 0, 1, 2])
                for h in range(8):
                    items.append((e_, h, var, offs, cnt["ev"] % 2))
                    cnt["ev"] += 1

            def att_S(it):
                e_, h, var, offs, par = it
                sb_ = par * 2
                qc = ring_col(e_)
                ktiles = sorted(set(T_k[((e_ + o) // 4) % 2] for o in offs), key=lambda t: t.name)
                for s_, o in enumerate(offs):
                    kc_ = ring_col(e_ + o)
                    P.op("pe", lambda e, s_=s_, kc_=kc_: e.matmul(
                        psall[:, sb_ * 512 + s_ * 128:sb_ * 512 + (s_ + 1) * 128],
                        lhsT=kT[:, h, kc_:kc_ + 128], rhs=qT[:, h, qc:qc + 128], start=True, stop=True),
                        reads=ktiles + [T_q[(e_ // 4) % 2]], writes=[PSB[sb_], PSB[sb_ + 1]])

            def att_mid(it):
                e_, h, var, offs, par = it
                sb_ = par * 2
                ns = len(offs)
                if state["tbv"] != var:
                    state["tbv"] = var
                    P.op("sp", lambda e: e.dma_start(out=ar2[:, 0:6144], in_=ebt[var]), reads=[T_ebt], writes=[T_tb], dma=D_tb)
                P.op("act", lambda e: e.activation(out=ets[par][:, 0:ns * 128], in_=psall[:, sb_ * 512:sb_ * 512 + ns * 128], func=AF.Exp),
                     reads=[PSB[sb_], PSB[sb_ + 1]], writes=[T_et[par]])
                P.op("dve", lambda e: e.tensor_tensor(out=ets[par][:, 0:ns * 128], in0=ets[par][:, 0:ns * 128],
                                                      in1=tb[:, h, 0:ns, :].rearrange("p s q -> p (s q)"), op=ALU.mult),
                     reads=[T_et[par], T_tb], writes=[T_et[par]])

            def att_PV(it):
                e_, h, var, offs, par = it
                pb = 4 + par
                ns = len(offs)
                for s_, o in enumerate(offs):
                    slot = (e_ + o) % 8
                    P.op("pe", lambda e, s_=s_, slot=slot: e.matmul(
                        psall[:, pb * 512:pb * 512 + 128], lhsT=vv[:, slot, h * 128:(h + 1) * 128],
                        rhs=ets[par][:, s_ * 128:(s_ + 1) * 128], start=(s_ == 0), stop=(s_ == ns - 1)),
                        reads=[T_v[slot], T_et[par]], writes=[PSB[pb]])
                for s_ in range(ns):
                    P.op("pe", lambda e, s_=s_: e.matmul(
                        psall[:, pb * 512 + 128:pb * 512 + 256], lhsT=ones_b[:, :],
                        rhs=ets[par][:, s_ * 128:(s_ + 1) * 128], start=(s_ == 0), stop=(s_ == ns - 1)),
                        reads=[T_ones, T_et[par]], writes=[PSB[pb]])

            def att_post(it):
                e_, h, var, offs, par = it
                pb = 4 + par
                P.op("dve", lambda e: e.reciprocal(out=rrs[par][:, :], in_=psall[:, pb * 512 + 128:pb * 512 + 256]),
                     reads=[PSB[pb]], writes=[T_rr[par]])
                tcol = (e_ - e0) * 128
                P.op("dve", lambda e: e.tensor_tensor(out=attnT[:, h, tcol:tcol + 128], in0=psall[:, pb * 512:pb * 512 + 128],
                                                      in1=rrs[par][:, :], op=ALU.mult),
                     reads=[PSB[pb], T_rr[par]], writes=[T_attn])

            def ln_stat(ct):
                y = ycv[:, ct, :]
                ti = ct % 2
                P.op("act", lambda e: e.activation(out=tmpf[ti][:, :], in_=y, func=AF.Square),
                     reads=[T_ycv[ct]], writes=[T_tmp[ti]])
                P.op("pe", lambda e: e.matmul(psf(6), lhsT=ones_f[:, :], rhs=y, start=(ct == 0), stop=(ct == 7)),
                     reads=[T_ones, T_ycv[ct]], writes=[PSB[6]])
                P.op("pe", lambda e: e.matmul(psf(7), lhsT=ones_f[:, :], rhs=tmpf[ti][:, :], start=(ct == 0), stop=(ct == 7)),
                     reads=[T_ones, T_tmp[ti]], writes=[PSB[7]])

            def ln_part():
                lnm, lnr = sigt[0], sigt[1]
                T_lnl = [T_sig[0], T_sig[1]]
                P.op("dve", lambda e: e.tensor_scalar(out=lnm[:, :], in0=psf(6), scalar1=1.0 / 1024, scalar2=None, op0=ALU.mult),
                     reads=[PSB[6]], writes=T_lnl)
                P.op("dve", lambda e: e.tensor_tensor(out=lnr[:, :], in0=lnm[:, :], in1=lnm[:, :], op=ALU.mult), reads=T_lnl, writes=T_lnl)
                P.op("dve", lambda e: e.scalar_tensor_tensor(out=lnr[:, :], in0=psf(7), scalar=1.0 / 1024, in1=lnr[:, :], op0=ALU.mult, op1=ALU.subtract),
                     reads=[PSB[7]] + T_lnl, writes=T_lnl)
                rstd_chain(P, lnr[:, :], lnr[:, :], tmpf[0][:, :], [T_sig[0], T_sig[1], T_tmp[0]], 1.0)
                for ct in range(8):
                    y = ycv[:, ct, :]
                    P.op("dve", lambda e, y=y: e.tensor_tensor(out=y, in0=y, in1=lnm[:, :], op=ALU.subtract), reads=T_lnl, writes=[T_ycv[ct]])
                    P.op("dve", lambda e, y=y: e.tensor_tensor(out=y, in0=y, in1=lnr[:, :], op=ALU.mult), reads=T_lnl, writes=[T_ycv[ct]])
                    P.op("act", lambda e, y=y, ct=ct: e.activation(out=cT[:, ct, :], in_=y, func=AF.Silu,
                                                                  scale=cvec[:, CV_LNG + ct:CV_LNG + ct + 1], bias=cvec[:, CV_LNB + ct:CV_LNB + ct + 1]),
                         reads=[T_ycv[ct], T_cv], writes=[T_cT])
                if g + 1 < NGRP:
                    P.op("pool", lambda e: e.tensor_copy(out=zT[:, :, 0:512], in_=zT[:, :, 512:1024]), reads=[T_z], writes=[T_z])

            def ga_proj(m):
                b = 6 + (m % 2)
                for kc in range(16):
                    w, wt = SS.lhs_block(OFF_C, m * 16 + kc)
                    P.op("pe", lambda e, w=w, kc=kc: e.matmul(psf(b), lhsT=w, rhs=uT[:, kc, 0:512], start=(kc == 0), stop=(kc == 15)),
                         reads=[wt, T_uT], writes=[PSB[b]])
                P.op("act", lambda e: e.activation(out=sga[:, m, :], in_=psf(b), func=AF.Sigmoid),
                     reads=[PSB[b]], writes=[T_ycv[m // 2]])

            att_S(items[0])
            for i, it in enumerate(items):
                if i + 1 < len(items):
                    att_S(items[i + 1])
                att_mid(it)
                att_PV(it)
                att_post(it)
                if 8 <= i < 16:
                    ln_stat(i - 8)
                if i == 16:
                    ln_part()
                if i >= 16:
                    ga_proj(i - 16)

        def phase_C(g):
            par = (g - 1) % 2
            if dbg and dbg_stop[0] == "E":
                if g == 1:
                    for k in range(5):
                        cast2(k)
            elif 1 <= g <= 5:
                cast2(g - 1)
            if g + 1 < NGRP:
                stage_x(g + 1, [0, 1])
            for m in range(16):
                b0 = 256 + m * 32
                bya = proj_lhs(OFF_C, b0, 8, lambda kc: attnT[:, kc, :], [T_attn])
                P.op("dve", lambda e, bya=bya, m=m: e.tensor_tensor(out=tmpf[0][:, :], in0=psf(bya), in1=sga[:, m, :], op=ALU.mult),
                     reads=[PSB[bya], T_ycv[m // 2]], writes=[T_tmp[0]])
                byc = proj_lhs(OFF_C, b0 + 8, 8, lambda kc: cT[:, kc, :], [T_cT])
                bgb = proj_lhs(OFF_C, b0 + 16, 16, lambda kc: uT[:, kc, 0:512], [T_uT])
                P.op("act", lambda e, bgb=bgb: e.activation(out=sigt[1][:, :], in_=psf(bgb), func=AF.Sigmoid), reads=[PSB[bgb]], writes=[T_sig[1]])
                P.op("dve", lambda e, byc=byc: e.tensor_tensor(out=tmpf[1][:, :], in0=psf(byc), in1=sigt[1][:, :], op=ALU.mult),
                     reads=[PSB[byc], T_sig[1]], writes=[T_tmp[1]])
                P.op("pool", lambda e, m=m: e.tensor_tensor(out=mT(m, par), in0=tmpf[0][:, :], in1=tmpf[1][:, :], op=ALU.add),
                     reads=[T_tmp[0], T_tmp[1]], writes=[T_mTp[par]])

        def phase_D(g):
            e0 = 4 * g - 2
            par = (g - 1) % 2
            if g + 1 < NGRP:
                P.op("act", lambda e: e.activation(out=uT[:, :, 0:256], in_=uT[:, :, 512:768], func=AF.Copy), reads=[T_uT], writes=[T_uT])
                for t in (2, 3):
                    early_loads[(g + 1, t)] = load_x(x_ext[4 * (g + 1) + t])
            for nb in range(4):
                for t in range(4):
                    b = 4 + (cnt["pv"] % 2)
                    cnt["pv"] += 1
                    for kc in range(16):
                        w, wt = SS.rhs_block(OFF_O, nb * 16 + kc)
                        P.op("pe", lambda e, b=b, w=w, kc=kc, t=t: e.matmul(psf(b), lhsT=mT(kc, par)[:, t * 128:(t + 1) * 128], rhs=w,
                                                                          start=(kc == 0), stop=(kc == 15)),
                             reads=[wt, T_mTp[par]], writes=[PSB[b]])
                    P.op("act", lambda e, b=b, t=t, nb=nb: e.activation(out=sqj1[:, :], in_=psf(b), func=AF.Square,
                                                                       accum_out=stat[:, t * 8 + 2 + nb:t * 8 + 3 + nb]),
                         reads=[PSB[b]], writes=[T_sqj1, T_stat[t]])
                    P.op("dve", lambda e, b=b, t=t, nb=nb: e.tensor_copy(out=o1[:, t, nb * 512:(nb + 1) * 512], in_=psf(b)),
                         reads=[PSB[b]], writes=[T_o1[t]])
            ssums = [stat[:, t * 8 + 6:t * 8 + 7] for t in range(4)]
            srss = [stat[:, t * 8 + 7:t * 8 + 8] for t in range(4)]
            for t in range(4):
                P.op("dve", lambda e, t=t: e.tensor_scalar(out=sjunk[:, 0:4], in0=stat[:, t * 8 + 2:t * 8 + 6], scalar1=1.0, scalar2=0.0,
                                                          op0=ALU.mult, op1=ALU.add, accum_out=ssums[t]),
                     reads=[T_stat[t]], writes=[T_stat[t]], late=True)
            for t in range(4):
                P.op("dve", lambda e, t=t: e.tensor_scalar(out=srss[t], in0=ssums[t], scalar1=1.0 / D, scalar2=EPS, op0=ALU.mult, op1=ALU.add),
                     reads=[T_stat[t]], writes=[T_stat[t]])
            for t in range(4):
                P.op("act", lambda e, t=t: e.activation(out=sjunk[:, 4 + t:5 + t], in_=srss[t], func=AF.Sqrt), reads=[T_stat[t]], writes=[T_stat[t]])
            for t in range(4):
                P.op("dve", lambda e, t=t: e.reciprocal(out=srss[t], in_=sjunk[:, 4 + t:5 + t]), reads=[T_stat[t]], writes=[T_stat[t]], late=True)
            for t in range(4):
                ptile = e0 + t - 2
                P.op("dve", lambda e, t=t: e.scalar_tensor_tensor(out=o1[:, t, :], in0=o1[:, t, :], scalar=srss[t], in1=gpost[:, :],
                                                                 op0=ALU.mult, op1=ALU.mult),
                     reads=[T_stat[t], T_gp], writes=[T_o1[t]])
                P.op("pool", lambda e, t=t: e.dma_start(out=o1[:, t, :], in_=x_ext[e0 + t], accum_op=ALU.add),
                     reads=[T_o1[t]], writes=[T_o1[t]], dma=D_acc[t])
                P.op("pool", lambda e, t=t, ptile=ptile: e.dma_start(out=hbuf[ptile], in_=o1[:, t, :]),
                     reads=[T_o1[t]], writes=[T_hbuf[ptile]], dma=D_h[t])

        def table_prologue():
            stage_tiles = T_o1 + T_ycv + [T_attn, T_cT]
            for v_ in range(5):
                P.op("sp", lambda e, v_=v_: e.dma_start(out=ar1[:, 0:6144], in_=tabs_d[v_]), writes=stage_tiles, dma=D_tb)
                P.op("act", lambda e: e.activation(out=ar2[:, 0:6144], in_=ar1[:, 0:6144], func=AF.Exp),
                     reads=stage_tiles, writes=[T_tb])
                P.op("sp", lambda e, v_=v_: e.dma_start(out=ebt[v_], in_=ar2[:, 0:6144]), reads=[T_tb], writes=[T_ebt], dma=D_ebt)

        stop = dbg_stop[0] if dbg else None
        table_prologue()
        diag_prologue()
        for g in range(NGRP):
            phase_A(g)
            if dbg and stop == "A" and g == 1:
                dump("dbg_q", qT[:, :, :], T_q)
                dump("dbg_k", kT[:, :, :], T_k)
                dump("dbg_v", vv[:, :, :], T_v)
                dump("dbg_z", zT[:, :, :], [T_z])
                break
            if g >= 1:
                phase_B(g)
                if dbg and stop == "B" and g == 1:
                    dump("dbg_attn", ar1_bf[:, 8192:12288], [T_attn])
                    dump("dbg_c", ar1_bf[:, 12288:16384], [T_cT])
                    break
                phase_C(g)
                if dbg and stop == "C" and g == 1:
                    pass
                    break
                phase_D(g)
                if dbg and stop == "D" and g == 1:
                    dump("dbg_h", ar1[:, :], T_o1)
                    dump("dbg_stat", stat[:, :], T_stat)
                    break
                if dbg and stop == "E" and g == dbg_stop[1]:
                    break
        P.emit()

    if dbg and stop in ("A", "B", "C", "D"):
        return nc

    with ExitStack() as es:
        def sb(name, shape, dt):
            return es.enter_context(nc.sbuf_tensor(name, shape, dt))

        def sem(name):
            return es.enter_context(nc.semaphore(name))

        esems = {e: sem("f_" + e) for e in ENGS}
        P = Prog(nc, esems)
        psall = es.enter_context(nc.psum_tensor("psall2", [128, 4096], F32))
        psall_bf = psall.bitcast(BF16)
        PSB = [Tile("psb%d" % i, exclusive=True) for i in range(8)]

        def psf(b, n=512):
            return psall[:, b * 512:b * 512 + n]

        ident = sb("ident2", [128, 128], BF16)
        cvec = sb("cvec2", [128, NCV], F32)
        gpost = sb("gpost2", [128, D], F32)
        uT = sb("u2T", [128, 16, 512], BF16)
        fT = sb("fT", [128, NJ, 512], BF16)
        hh = [sb("hh%d" % i, [128, 4, D], F32) for i in range(2)]
        o2 = sb("o2", [128, 4, D], F32)
        slabs = [sb("slabf%d" % i, [128, 4096], BF16) for i in range(NSB)]
        utms = [sb("utm2_%d" % i, [128, D], BF16) for i in range(2)]
        sqj = sb("sqj", [128, 512], BF16)
        sigt = [sb("sigf%d" % i, [128, 512], F32) for i in range(2)]
        stat = sb("stat2", [128, 64], F32)
        sjunk = sb("sjunk2", [128, 8], F32)

        T_const = Tile("const")
        T_cv = Tile("cvec")
        T_gp = Tile("gpost")
        T_uT = Tile("uT")
        T_fT = [Tile("fT%d" % j) for j in range(NJ)]
        T_h = [[Tile("h%d_%d" % (j, i)) for i in range(4)] for j in range(2)]
        T_o2 = [Tile("o2_%d" % i) for i in range(4)]
        T_slab = [Tile("slab%d" % i) for i in range(NSB)]
        T_utms = [Tile("utm%d" % i) for i in range(2)]
        T_sqj = Tile("sqj")
        T_sig = [Tile("sig%d" % i) for i in range(2)]
        T_stat = [Tile("stat%d" % i) for i in range(8)]
        T_w2 = Tile("wbf_s2")
        D_slab = [P.new_dma_sem(sem("g_slab%d" % i)) for i in range(NSB)]
        D_hl = [P.new_dma_sem(sem("g_h%d" % i)) for i in range(4)]
        D_o = [P.new_dma_sem(sem("g_o%d" % i)) for i in range(4)]
        D_consts = [P.new_dma_sem(sem("g_const%d" % i)) for i in range(3)]

        P.op("pool", lambda e: e.dma_start(out=ident[:, :], in_=ident_d[:, :]), writes=[T_const], dma=D_consts[0])
        P.op("pool", lambda e: e.dma_start(out=cvec[:, :], in_=cvec_d[:, :]), writes=[T_cv], dma=D_consts[1])
        P.op("pool", lambda e: e.dma_start(out=gpost[:, :], in_=gpost_d[1]), writes=[T_gp], dma=D_consts[2])
        SS = SlabStream(P, wbf, slabs, T_slab, D_slab, {})
        cnt = {"pt": 0, "ps": 0, "sig": 0}

        NBLK2 = dbg_stop[1] if (dbg and dbg_stop[0] == "E") else 8

        def load_h(blk):
            hb = blk % 2
            for t in range(4):
                P.op("act", lambda e, t=t: e.dma_start(out=hh[hb][:, t, :], in_=hbuf[4 * blk + t]), writes=[T_h[hb][t]], dma=D_hl[t])

        def norm_T(blk):
            hb = blk % 2
            for t in range(4):
                sq = stat[:, 32 + t * 4:32 + t * 4 + 1]
                rs_ = stat[:, 32 + t * 4 + 1:32 + t * 4 + 2]
                tmp_ = stat[:, 32 + t * 4 + 2:32 + t * 4 + 3]
                ui = t % 2
                utm = utms[ui]
                T_utm = T_utms[ui]
                P.op("act", lambda e, t=t, sq=sq, utm=utm: e.activation(out=utm[:, :], in_=hh[hb][:, t, :], func=AF.Square, accum_out=sq),
                     reads=[T_h[hb][t]], writes=[T_utm, T_stat[4 + t]])
                rstd_chain(P, sq, rs_, tmp_, [T_stat[4 + t]], 1.0 / D)
                P.op("dve", lambda e, t=t, rs_=rs_, utm=utm: e.tensor_scalar(out=utm[:, :], in0=hh[hb][:, t, :], scalar1=rs_, scalar2=None, op0=ALU.mult),
                     reads=[T_h[hb][t], T_stat[4 + t]], writes=[T_utm])
                for half in range(2):
                    b = 6 + half
                    for j in range(8):
                        kc = half * 8 + j
                        P.op("pe", lambda e, b=b, j=j, kc=kc, utm=utm: e.transpose(
                            out=psall_bf[:, b * 1024 + j * 128:b * 1024 + (j + 1) * 128],
                            in_=utm[:, kc * 128:(kc + 1) * 128], identity=ident[:, :]),
                            reads=[T_utm, T_const], writes=[PSB[b]])
                    for j in range(8):
                        kc = half * 8 + j
                        src = psall_bf[:, b * 1024 + j * 128:b * 1024 + (j + 1) * 128]
                        dst = uT[:, kc, t * 128:(t + 1) * 128]
                        gs = cvec[:, CV_GPRE2 + kc:CV_GPRE2 + kc + 1]
                        if half == 0:
                            P.op("act", lambda e, src=src, dst=dst, gs=gs: e.activation(out=dst, in_=src, func=AF.Copy, scale=gs),
                                 reads=[PSB[b], T_cv], writes=[T_uT])
                        else:
                            P.op("dve", lambda e, src=src, dst=dst, gs=gs: e.tensor_scalar(out=dst, in0=src, scalar1=gs, scalar2=None, op0=ALU.mult),
                                 reads=[PSB[b], T_cv], writes=[T_uT])

        load_h(0)
        norm_T(0)
        for blk in range(NBLK2):
            hb = blk % 2
            for j in range(NJ):
                bs = []
                for gu in range(2):
                    b = cnt["ps"] % 6
                    cnt["ps"] += 1
                    for kc in range(16):
                        w, wt = SS.lhs_block(OFF_GU, (2 * j + gu) * 16 + kc)
                        P.op("pe", lambda e, b=b, w=w, kc=kc: e.matmul(psf(b), lhsT=w, rhs=uT[:, kc, :], start=(kc == 0), stop=(kc == 15)),
                             reads=[wt, T_uT], writes=[PSB[b]])
                    bs.append(b)
                si = cnt["sig"] % 2
                cnt["sig"] += 1
                P.op("act", lambda e, b=bs[0], si=si: e.activation(out=sigt[si][:, :], in_=psf(b), func=AF.Silu), reads=[PSB[bs[0]]], writes=[T_sig[si]])
                P.op("dve", lambda e, b=bs[1], si=si, j=j: e.tensor_tensor(out=fT[:, j, :], in0=psf(b), in1=sigt[si][:, :], op=ALU.mult),
                     reads=[PSB[bs[1]], T_sig[si]], writes=[T_fT[j]])
            if blk + 1 < NBLK2:
                load_h(blk + 1)
            for nq in range(4):
                banks = [(nq * 4 + t) % 6 for t in range(4)]
                for kc in range(NJ):
                    w, wt = SS.rhs_block(OFF_D, nq * NJ + kc)
                    for t in range(4):
                        b = banks[t]
                        P.op("pe", lambda e, b=b, w=w, kc=kc, t=t: e.matmul(psf(b), lhsT=fT[:, kc, t * 128:(t + 1) * 128], rhs=w,
                                                                          start=(kc == 0), stop=(kc == NJ - 1)),
                             reads=[wt, T_fT[kc]], writes=[PSB[b]])
                for t in range(4):
                    b = banks[t]
                    P.op("act", lambda e, b=b, t=t, nq=nq: e.activation(out=sqj[:, :], in_=psf(b), func=AF.Square,
                                                                       accum_out=stat[:, t * 8 + 2 + nq:t * 8 + 3 + nq]),
                         reads=[PSB[b]], writes=[T_sqj, T_stat[t]])
                    P.op("dve", lambda e, b=b, t=t, nq=nq: e.tensor_copy(out=o2[:, t, nq * 512:(nq + 1) * 512], in_=psf(b)),
                         reads=[PSB[b]], writes=[T_o2[t]])
                if nq == 1 and blk + 1 < NBLK2:
                    norm_T(blk + 1)
            ssums = [stat[:, t * 8 + 6:t * 8 + 7] for t in range(4)]
            srss = [stat[:, t * 8 + 7:t * 8 + 8] for t in range(4)]
            for t in range(4):
                P.op("dve", lambda e, t=t: e.tensor_scalar(out=sjunk[:, 0:4], in0=stat[:, t * 8 + 2:t * 8 + 6], scalar1=1.0, scalar2=0.0,
                                                          op0=ALU.mult, op1=ALU.add, accum_out=ssums[t]),
                     reads=[T_stat[t]], writes=[T_stat[t]], late=True)
            for t in range(4):
                P.op("dve", lambda e, t=t: e.tensor_scalar(out=srss[t], in0=ssums[t], scalar1=1.0 / D, scalar2=EPS, op0=ALU.mult, op1=ALU.add),
                     reads=[T_stat[t]], writes=[T_stat[t]])
            for t in range(4):
                P.op("act", lambda e, t=t: e.activation(out=sjunk[:, 4 + t:5 + t], in_=srss[t], func=AF.Sqrt), reads=[T_stat[t]], writes=[T_stat[t]])
            for t in range(4):
                P.op("dve", lambda e, t=t: e.reciprocal(out=srss[t], in_=sjunk[:, 4 + t:5 + t]), reads=[T_stat[t]], writes=[T_stat[t]], late=True)
            for t in range(4):
                P.op("dve", lambda e, t=t: e.scalar_tensor_tensor(out=o2[:, t, :], in0=o2[:, t, :], scalar=srss[t], in1=gpost[:, :],
                                                                 op0=ALU.mult, op1=ALU.mult),
                     reads=[T_stat[t], T_gp], writes=[T_o2[t]])
                P.op("dve", lambda e, t=t, hb=hb: e.tensor_tensor(out=o2[:, t, :], in0=o2[:, t, :], in1=hh[hb][:, t, :], op=ALU.add),
                     reads=[T_h[hb][t]], writes=[T_o2[t]])
                P.op("act", lambda e, t=t, blk=blk: e.dma_start(out=out_d[4 * blk + t], in_=o2[:, t, :]),
                     reads=[T_o2[t]], writes=[], dma=D_o[t])
        P.emit()
    return nc


dbg_stop = [None, 1]


def _lhs_blocks(W, cols_tiles):
    K = W.shape[0]
    kc = K // 128
    out = np.empty((len(cols_tiles) * kc, 128, 128), np.float32)
    i = 0
    for c in cols_tiles:
        blk = W[:, c:c + 128].reshape(kc, 128, 128)
        out[i:i + kc] = blk
        i += kc
    return out


def _pack_lhs(blocks):
    nb = blocks.shape[0]
    assert nb % 32 == 0
    return np.ascontiguousarray(blocks.reshape(nb // 32, 32, 128, 128).transpose(0, 2, 1, 3)).reshape(nb // 32 * 128, 4096)


def _rhs_blocks(W, col_starts):
    K = W.shape[0]
    kc = K // 128
    out = np.empty((len(col_starts) * kc, 128, 512), np.float32)
    i = 0
    for c in col_starts:
        out[i:i + kc] = W[:, c:c + 512].reshape(kc, 128, 512)
        i += kc
    return out


def _pack_rhs(blocks):
    nb = blocks.shape[0]
    assert nb % 8 == 0
    return np.ascontiguousarray(blocks.reshape(nb // 8, 8, 128, 512).transpose(0, 2, 1, 3)).reshape(nb // 8 * 128, 4096)


def _build_wall(w_in, w_attn_o, w_conv_o, w_out, w_gate_up, w_down):
    parts = []
    cols = []
    for j in range(8):
        cols += [3072 + 128 * j, 4096 + 128 * j]
    cols += [128 * m for m in range(8)] + [1024 + 128 * m for m in range(8)]
    parts.append(_pack_lhs(_lhs_blocks(w_in, cols)))
    parts.append(_pack_rhs(_rhs_blocks(w_in, [2048, 2560])))
    blks = [_lhs_blocks(w_in, [5120 + 128 * m for m in range(16)])]
    for m in range(16):
        blks.append(_lhs_blocks(w_attn_o, [128 * m]))
        blks.append(_lhs_blocks(w_conv_o, [128 * m]))
        blks.append(_lhs_blocks(w_in, [7168 + 128 * m]))
    parts.append(_pack_lhs(np.concatenate(blks, 0)))
    parts.append(_pack_rhs(_rhs_blocks(w_out, [0, 512, 1024, 1536])))
    cols = []
    for j in range(NJ):
        cols += [128 * j, DFF + 128 * j]
    parts.append(_pack_lhs(_lhs_blocks(w_gate_up, cols)))
    parts.append(_pack_rhs(_rhs_blocks(w_down, [0, 512, 1024, 1536])))
    wall = np.concatenate(parts, 0)
    assert wall.shape == (NSLAB * 128, 4096), wall.shape
    return wall


def _build_tables(rpb, qd):
    NEG = np.float32(-30000.0)
    tabs = np.full((5, 128, 8, 6, 128), NEG, np.float32)
    kk = np.arange(128)
    kr2, kcol = kk // 64, kk % 64
    qq = np.arange(128)
    qr2, qcol = qq // 64, qq % 64
    for vi, p in enumerate([0, 1, 2, 30, 31]):
        offs = {0: [-2, -1, 0, 1, 2, 3], 4: [-3, -2, -1, 0, 1, 2]}.get(vi, [-2, -1, 0, 1, 2])
        r0 = 64 * qd + 2 * p
        for s, o in enumerate(offs):
            krow = (r0 + 2 * o + kr2)[:, None]
            qrow = (r0 + qr2)[None, :]
            rs = np.clip(qrow - 4, 0, 248)
            cs = np.clip(qcol - 8, 0, 48)[None, :]
            kc_ = kcol[:, None]
            ok = (krow >= rs) & (krow < rs + 8) & (kc_ >= cs) & (kc_ < cs + 16) & (krow >= 0) & (krow < 256)
            dr = np.clip(krow - qrow + 7, 0, 14)
            dc = np.clip(kc_ - qcol[None, :] + 15, 0, 30)
            for h in range(8):
                vals = rpb[h][dr, dc]
                tabs[vi, :, h, s, :] = np.where(ok, vals, NEG)
    return tabs.reshape(5, 128, 6144)


def _build_cvec(mix_pre_g, ffn_pre_g, w_dw, b_dw, ln_g, ln_b):
    cv = np.zeros((128, NCV), np.float32)
    cv[:, CV_GPRE1:CV_GPRE1 + 16] = mix_pre_g.reshape(16, 128).T
    cv[:, CV_GPRE2:CV_GPRE2 + 16] = ffn_pre_g.reshape(16, 128).T
    cv[:, CV_WDW:CV_WDW + 248] = w_dw.reshape(31, 8, 128).transpose(2, 1, 0).reshape(128, 248)
    cv[:, CV_BDW:CV_BDW + 8] = b_dw.reshape(8, 128).T
    cv[:, CV_LNG:CV_LNG + 8] = ln_g.reshape(8, 128).T
    cv[:, CV_LNB:CV_LNB + 8] = ln_b.reshape(8, 128).T
    return cv


def _prep_inputs(x, mix_pre_g, mix_post_g, w_in, rpb, w_attn_o, w_dw, b_dw, conv_ln_g, conv_ln_b,
                 w_conv_o, w_out, ffn_pre_g, ffn_post_g, w_gate_up, w_down):
    f = lambda a: np.asarray(a, np.float32)
    x = f(x)
    wall = _build_wall(f(w_in)[0], f(w_attn_o)[0], f(w_conv_o)[0], f(w_out)[0], f(w_gate_up)[0], f(w_down)[0])
    cv = _build_cvec(f(mix_pre_g)[0], f(ffn_pre_g)[0], f(w_dw)[0], f(b_dw)[0], f(conv_ln_g)[0], f(conv_ln_b)[0])
    gp = np.stack([np.broadcast_to(f(mix_post_g)[0], (128, D)), np.broadcast_to(f(ffn_post_g)[0], (128, D))]).astype(np.float32)
    gp = np.ascontiguousarray(gp)
    ident = np.eye(128, dtype=np.float32)
    tabs_q = [_build_tables(f(rpb)[0], qd) for qd in range(4)]
    in_maps = []
    for c in range(NCORES):
        b, qd = c // 4, c % 4
        xe = np.zeros((NT_EXT * 128, D), np.float32)
        lo = 4096 * qd - 256
        hi = 4096 * qd + 4096 + 256
        slo, shi = max(lo, 0), min(hi, 16384)
        xe[slo - lo:shi - lo] = x[b, slo:shi]
        in_maps.append({"x_ext": xe.reshape(NT_EXT, 128, D), "wall": wall, "cvec": cv, "gpost": gp,
                        "tabs": tabs_q[qd], "ident": ident})
    return in_maps


def kernel(**inputs):
    in_maps = _prep_inputs(**inputs)
    nc = build_program()
    res = run_bass_kernel_spmd(nc, in_maps, core_ids=list(range(NCORES)))
    out = np.empty((2, 16384, D), np.float32)
    for c in range(NCORES):
        b, qd = c // 4, c % 4
        out[b, 4096 * qd:4096 * (qd + 1)] = np.asarray(res.results[c]["out"]).reshape(4096, D)
    return out
```

```python
import numpy as np
import concourse.bass as bass
import concourse.mybir as mybir
from concourse.bass_utils import run_bass_kernel_spmd

F32 = mybir.dt.float32
BF16 = mybir.dt.bfloat16
AF = mybir.ActivationFunctionType
ALU = mybir.AluOpType
AX = mybir.AxisListType

D = 2048
NCORES = 8
TOK_CORE = 4096
NT_EXT = 36
NGRP = 9
DFF = 5632
NJ = 44
EPS = 1e-6
QSCALE = 128 ** -0.5
S_A, S_V, S_C, S_O, S_GU, S_D = 16, 4, 24, 8, 44, 22
OFF_A = 0
OFF_V = OFF_A + S_A
OFF_C = OFF_V + S_V
OFF_O = OFF_C + S_C
OFF_GU = OFF_O + S_O
OFF_D = OFF_GU + S_GU
NSLAB = OFF_D + S_D
NS1 = OFF_GU
NCV = 16 + 16 + 8 * 31 + 24
CV_GPRE1, CV_GPRE2, CV_WDW, CV_BDW, CV_LNG, CV_LNB = 0, 16, 32, 32 + 248, 32 + 256, 32 + 264
NSB = 3


class Tile:
    def __init__(self, name, exclusive=False):
        self.name = name
        self.exclusive = exclusive
        self.writer = None
        self.readers = []
        self.aliases = []

    def group(self):
        return [self] + self.aliases


def alias(a, b):
    a.aliases.append(b)
    b.aliases.append(a)


class DmaSem:
    def __init__(self, sem):
        self.sem = sem
        self.count = 0


class Op:
    __slots__ = ("eng", "fn", "deps", "dma", "semval", "needs_inc", "idx", "late")

    def __init__(self, eng, fn, dma):
        self.eng = eng
        self.fn = fn
        self.dma = dma
        self.deps = []
        self.semval = None
        self.needs_inc = False
        self.idx = -1
        self.late = False


ENGS = ["pe", "act", "dve", "pool", "sp"]


class Prog:
    def __init__(self, nc, esems):
        self.nc = nc
        self.ops = {e: [] for e in ENGS}
        self.esem = esems
        self.dmasems = []

    def new_dma_sem(self, sem):
        d = DmaSem(sem)
        self.dmasems.append(d)
        return d

    def op(self, eng, fn, reads=(), writes=(), dma=None, late=False):
        o = Op(eng, fn, dma)
        o.late = late
        writes = list(writes) + [t for t in reads if t.exclusive]
        reads = [t for t in reads if not t.exclusive]
        raw = []
        for t in reads:
            for tt in t.group():
                if tt.writer is not None:
                    raw.append(tt.writer)
        for t in writes:
            for tt in t.group():
                if tt.writer is not None:
                    raw.append(tt.writer)
                raw.extend(tt.readers)
        best = {}
        deps = []
        seen = set()
        for d in raw:
            if id(d) in seen:
                continue
            seen.add(id(d))
            if d.dma is not None:
                deps.append(d)
            else:
                if d.eng == eng and dma is None and not d.late:
                    continue
                b = best.get(d.eng)
                if b is None or d.idx > b.idx:
                    best[d.eng] = d
        deps.extend(best.values())
        o.deps = deps
        for d in deps:
            if d.dma is None:
                d.needs_inc = True
        for t in reads:
            t.readers.append(o)
        for t in writes:
            t.writer = o
            t.readers = []
        if dma is not None:
            dma.count += 16
            o.semval = dma.count
        o.idx = len(self.ops[eng])
        self.ops[eng].append(o)
        return o

    def emit(self):
        nc = self.nc
        for e in ENGS:
            c = 0
            for o in self.ops[e]:
                if o.dma is None and o.needs_inc:
                    c += 1
                    o.semval = c
            assert c < 60000, (e, c)
        for d in self.dmasems:
            assert d.count < 60000

        def run(ename, eng):
            known = {}
            for o in self.ops[ename]:
                need = {}
                for d in o.deps:
                    s = d.dma.sem if d.dma is not None else self.esem[d.eng]
                    k = id(s)
                    if k not in need or need[k][1] < d.semval:
                        need[k] = (s, d.semval)
                for k, (s, v) in need.items():
                    if known.get(k, 0) >= v:
                        continue
                    eng.wait_ge(s, v)
                    known[k] = v
                ins = o.fn(eng)
                if o.dma is not None:
                    ins.then_inc(o.dma.sem, 16)
                elif o.needs_inc:
                    ins.then_inc(self.esem[ename], 1)
            if ename == "sp":
                for d in self.dmasems:
                    if d.count > 0:
                        eng.wait_ge(d.sem, d.count)

        with nc.Block() as block:
            @block.tensor
            def _(e):
                run("pe", e)

            @block.scalar
            def _(e):
                run("act", e)

            @block.vector
            def _(e):
                run("dve", e)

            @block.gpsimd
            def _(e):
                run("pool", e)

            @block.sync
            def _(e):
                run("sp", e)


class SlabStream:
    def __init__(self, P, wbf, bufs, buf_tiles, dsems, src_tile):
        self.P = P
        self.wbf = wbf
        self.bufs = bufs
        self.tiles = buf_tiles
        self.dsems = dsems
        self.src_tile = src_tile
        self.loaded = {}
        self.counter = 0
        self.resident = [None] * len(bufs)

    def get(self, slab):
        if slab in self.loaded:
            return self.loaded[slab]
        i = self.counter % len(self.bufs)
        self.counter += 1
        old = self.resident[i]
        if old is not None:
            del self.loaded[old]
        self.resident[i] = slab
        self.loaded[slab] = i
        buf = self.bufs[i]
        src = self.wbf[slab * 128:(slab + 1) * 128, :]
        self.P.op("sp", lambda e, buf=buf, src=src: e.dma_start(out=buf[:, :], in_=src),
                  reads=([self.src_tile[slab]] if slab in self.src_tile else []), writes=[self.tiles[i]], dma=self.dsems[i])
        return i

    def lhs_block(self, base, b):
        i = self.get(base + b // 32)
        o = (b % 32) * 128
        return self.bufs[i][:, o:o + 128], self.tiles[i]

    def rhs_block(self, base, b):
        i = self.get(base + b // 8)
        o = (b % 8) * 512
        return self.bufs[i][:, o:o + 512], self.tiles[i]


def rstd_chain(P, src, dst, tmp, tiles, scale):
    P.op("dve", lambda e: e.tensor_scalar(out=dst, in0=src, scalar1=scale, scalar2=EPS, op0=ALU.mult, op1=ALU.add),
         reads=tiles, writes=tiles)
    P.op("act", lambda e: e.activation(out=tmp, in_=dst, func=AF.Sqrt), reads=tiles, writes=tiles)
    P.op("dve", lambda e: e.reciprocal(out=dst, in_=tmp), reads=tiles, writes=tiles, late=True)

def build_program(dbg=None):
    nc = bass.Bass("TRN2", target_bir_lowering=False)
    x_ext = nc.dram_tensor("x_ext", [NT_EXT, 128, D], F32, kind="ExternalInput").ap()
    wall = nc.dram_tensor("wall", [NSLAB * 128, 4096], F32, kind="ExternalInput").ap()
    cvec_d = nc.dram_tensor("cvec", [128, NCV], F32, kind="ExternalInput").ap()
    gpost_d = nc.dram_tensor("gpost", [2, 128, D], F32, kind="ExternalInput").ap()
    tabs_d = nc.dram_tensor("tabs", [5, 128, 6144], F32, kind="ExternalInput").ap()
    ident_d = nc.dram_tensor("ident", [128, 128], F32, kind="ExternalInput").ap()
    out_d = nc.dram_tensor("out", [32, 128, D], F32, kind="ExternalOutput").ap()
    wbf = nc.dram_tensor("wbf", [(NSLAB + 4) * 128, 4096], BF16, kind="Internal").ap()
    ebt = nc.dram_tensor("ebt", [5, 128, 6144], BF16, kind="Internal").ap()
    hbuf = nc.dram_tensor("hbuf", [32, 128, D], F32, kind="Internal").ap()
    dbg_d = {}
    if dbg:
        for name, shape, dt in dbg:
            dbg_d[name] = nc.dram_tensor(name, shape, dt, kind="ExternalOutput").ap()

    from contextlib import ExitStack

    with ExitStack() as es:
        def sb(name, shape, dt):
            return es.enter_context(nc.sbuf_tensor(name, shape, dt))

        def sem(name):
            return es.enter_context(nc.semaphore(name))

        esems = {e: sem("e_" + e) for e in ENGS}
        P = Prog(nc, esems)

        psall = es.enter_context(nc.psum_tensor("psall", [128, 4096], F32))
        psall_bf = psall.bitcast(BF16)
        PSB = [Tile("psb%d" % i, exclusive=True) for i in range(8)]

        def psf(b, n=512, nb=1):
            return psall[:, b * 512:b * 512 + n]

        ident = sb("ident_sb", [128, 128], BF16)
        ones_f = sb("ones_f", [128, 128], F32)
        ones_b = sb("ones_b", [128, 128], BF16)
        cvec = sb("cvec_sb", [128, NCV], F32)
        gpost = sb("gpost_sb", [128, D], F32)
        uT = sb("uT", [128, 16, 768], BF16)
        qT = sb("qT", [128, 8, 1024], BF16)
        kT = sb("kT", [128, 8, 1024], BF16)
        vv = sb("vv", [128, 8, 1024], BF16)
        zT = sb("zT", [128, 8, 1024], BF16)
        slabs = [sb("slab%d" % i, [128, 4096], BF16) for i in range(NSB)]
        xs = [sb("xs%d" % i, [128, D], F32) for i in range(2)]
        utms = [sb("utm%d" % i, [128, D], BF16) for i in range(2)]
        utm = utms[0]
        ar1 = sb("ar1", [128, 8192], F32)
        ar2 = sb("ar2", [128, 8192], BF16)
        sigt = [sb("sigt%d" % i, [128, 512], F32) for i in range(2)]
        tmpf = [sb("tmpf%d" % i, [128, 512], F32) for i in range(2)]
        stat = sb("stat", [128, 64], F32)
        sjunk = sb("sjunk", [128, 8], F32)
        rrs = [sb("rr%d" % i, [128, 128], F32) for i in range(2)]
        sqj1 = sb("sqj1", [128, 512], BF16)

        ycv = ar1[:, 0:4096].rearrange("p (c t) -> p c t", c=8)
        ar1_bf = ar1.bitcast(BF16)
        attnT = ar1_bf[:, 8192:12288].rearrange("p (c t) -> p c t", c=8)
        cT = ar1_bf[:, 12288:16384].rearrange("p (c t) -> p c t", c=8)
        sga = ar1_bf[:, 0:8192].rearrange("p (c t) -> p c t", c=16)
        o1 = ar1[:, :].rearrange("p (t d) -> p t d", t=4)
        tb = ar2[:, 0:6144].rearrange("p (h s q) -> p h s q", h=8, s=6)
        ets = [ar2[:, 6144 + i * 768:6144 + (i + 1) * 768] for i in range(2)]

        def mT(m, par):
            return qT[:, m, par * 512:(par + 1) * 512] if m < 8 else kT[:, m - 8, par * 512:(par + 1) * 512]

        T_const = Tile("const")
        T_uT = Tile("uT")
        T_q = [Tile("q%d" % i) for i in range(2)]
        T_k = [Tile("k%d" % i) for i in range(2)]
        T_v = [Tile("v%d" % i) for i in range(8)]
        T_z = Tile("z")
        T_slab = [Tile("slab%d" % i) for i in range(NSB)]
        T_xs = [Tile("xs%d" % i) for i in range(2)]
        T_utms = [Tile("utm%d" % i) for i in range(2)]
        T_utm = T_utms[0]
        T_ycv = [Tile("ycv%d" % i) for i in range(8)]
        T_attn = Tile("attnT")
        T_cT = Tile("cT")
        T_o1 = [Tile("o1_%d" % i) for i in range(4)]
        for t in T_o1:
            for u in T_ycv + [T_attn, T_cT]:
                alias(t, u)
        T_tb = Tile("tb")
        T_et = [Tile("et%d" % i) for i in range(2)]
        T_mTp = [Tile("mergedT%d" % i) for i in range(2)]
        for i in range(2):
            alias(T_mTp[i], T_q[i])
            alias(T_mTp[i], T_k[i])
        T_sig = [Tile("sig%d" % i) for i in range(2)]
        T_tmp = [Tile("tmp%d" % i) for i in range(2)]
        T_stat = [Tile("stat%d" % i) for i in range(8)]
        T_rr = [Tile("rr%d" % i) for i in range(2)]
        T_dg = [Tile("dg%d" % i) for i in range(4)]
        T_sqj1 = Tile("sqj1")
        T_w1 = Tile("wbf_s1")
        T_w2 = Tile("wbf_s2")
        T_ebt = Tile("ebt")
        T_hbuf = [Tile("hbuf%d" % i) for i in range(32)]
        T_dbg = Tile("dbg")

        D_slab = [P.new_dma_sem(sem("d_slab%d" % i)) for i in range(NSB)]
        D_xs = [P.new_dma_sem(sem("d_xs%d" % i)) for i in range(2)]
        D_c1 = [P.new_dma_sem(sem("d_cast%d" % i)) for i in range(4)]
        D_consts = [P.new_dma_sem(sem("d_const%d" % i)) for i in range(3)]
        D_tb = P.new_dma_sem(sem("d_tb"))
        D_dgw = P.new_dma_sem(sem("d_dgw"))
        D_ebt = P.new_dma_sem(sem("d_ebt"))
        D_h = [P.new_dma_sem(sem("d_h%d" % i)) for i in range(4)]
        D_acc = [P.new_dma_sem(sem("d_acc%d" % i)) for i in range(4)]
        early_loads = {}
        D_dbg = P.new_dma_sem(sem("d_dbg"))

        slab_tile = {}

        def cast(bounds, dsem):
            for k in range(len(bounds) - 1):
                s_, e_ = bounds[k], bounds[k + 1]
                tt = Tile("cast%d" % s_)
                P.op("pool", lambda e, s_=s_, e_=e_: e.dma_start(out=wbf[s_ * 128:e_ * 128, :], in_=wall[s_ * 128:e_ * 128, :]),
                     writes=[tt], dma=dsem[k])
                for j in range(s_, e_):
                    slab_tile[j] = tt
        cast([0, 8, 20, 36, NS1], D_c1)
        P.op("pool", lambda e: e.dma_start(out=ident[:, :], in_=ident_d[:, :]), writes=[T_const], dma=D_consts[0])
        T_cv = Tile("cvec")
        P.op("pool", lambda e: e.dma_start(out=cvec[:, :], in_=cvec_d[:, :]), writes=[T_cv], dma=D_consts[1])
        T_gp = Tile("gpost")
        P.op("pool", lambda e: e.dma_start(out=gpost[:, :], in_=gpost_d[0]), writes=[T_gp], dma=D_consts[2])
        T_ones = Tile("ones")
        P.op("dve", lambda e: e.memset(ones_f[:, :], 1.0), writes=[T_ones])
        P.op("dve", lambda e: e.memset(ones_b[:, :], 1.0), writes=[T_ones])
        P.op("pool", lambda e: e.memset(zT[:, :, :], 0.0), writes=[T_z])
        D_c2s = P.new_dma_sem(sem("d_cst2all"))
        c2_pieces = [(s_, min(s_ + 2, NSLAB)) for s_ in range(NS1, NSLAB, 2)]

        def cast2_piece(n=1):
            for _ in range(n):
                if c2_pieces:
                    s_, e_ = c2_pieces.pop(0)
                    P.op("pool", lambda e, s_=s_, e_=e_: e.dma_start(out=wbf[s_ * 128:e_ * 128, :], in_=wall[s_ * 128:e_ * 128, :]),
                         writes=[Tile("cast2_%d" % s_)], dma=D_c2s)

        SS = SlabStream(P, wbf, slabs, T_slab, D_slab, slab_tile)

        cnt = {"xs": 0, "pt": 0, "ps": 0, "sig": 0, "stat": 0, "pv": 0, "ev": 0, "ctmp": 0, "dg": 0}

        def load_x(src_ap):
            i = cnt["xs"] % 2
            cnt["xs"] += 1
            P.op("act", lambda e: e.dma_start(out=xs[i][:, :], in_=src_ap), writes=[T_xs[i]], dma=D_xs[i])
            return i

        def pre_x(i):
            ui = cnt["stat"] % 2
            utm = utms[ui]
            T_utm = T_utms[ui]
            si = cnt["stat"] % 8
            cnt["stat"] += 1
            sq = stat[:, si * 8:si * 8 + 1]
            rs_ = stat[:, si * 8 + 1:si * 8 + 2]
            P.op("act", lambda e: e.activation(out=utm[:, :], in_=xs[i][:, :], func=AF.Square, accum_out=sq),
                 reads=[T_xs[i]], writes=[T_utm, T_stat[si]])
            rstd_chain(P, sq, rs_, stat[:, si * 8 + 2:si * 8 + 3], [T_stat[si]], 1.0 / D)
            P.op("dve", lambda e: e.tensor_scalar(out=utm[:, :], in0=xs[i][:, :], scalar1=rs_, scalar2=None, op0=ALU.mult),
                 reads=[T_xs[i], T_stat[si]], writes=[T_utm])
            return ui

        def post_x(ui, col, gcol0):
            utm = utms[ui]
            T_utm = T_utms[ui]
            for half in range(2):
                b = 6 + half
                for j in range(8):
                    kc = half * 8 + j
                    P.op("pe", lambda e, b=b, j=j, kc=kc: e.transpose(
                        out=psall_bf[:, b * 1024 + j * 128:b * 1024 + (j + 1) * 128],
                        in_=utm[:, kc * 128:(kc + 1) * 128], identity=ident[:, :]),
                        reads=[T_utm, T_const], writes=[PSB[b]])
                for j in range(8):
                    kc = half * 8 + j
                    src = psall_bf[:, b * 1024 + j * 128:b * 1024 + (j + 1) * 128]
                    dst = uT[:, kc, col:col + 128]
                    gs = cvec[:, gcol0 + kc:gcol0 + kc + 1]
                    if half == 0:
                        P.op("act", lambda e, src=src, dst=dst, gs=gs: e.activation(out=dst, in_=src, func=AF.Copy, scale=gs),
                             reads=[PSB[b], T_cv], writes=[T_uT])
                    else:
                        P.op("dve", lambda e, src=src, dst=dst, gs=gs: e.tensor_scalar(out=dst, in0=src, scalar1=gs, scalar2=None, op0=ALU.mult),
                             reads=[PSB[b], T_cv], writes=[T_uT])

        staged = {}

        def stage_x(g, tiles):
            for t in tiles:
                i = load_x(x_ext[4 * g + t])
                staged[(g, t)] = pre_x(i)

        def next_ps():
            b = cnt["ps"] % 4
            cnt["ps"] += 1
            return b

        def proj_lhs(base, b0, nk, rhs_fn, rhs_tiles, n=512):
            b = next_ps()
            for kc in range(nk):
                w, wt = SS.lhs_block(base, b0 + kc)
                rhs = rhs_fn(kc)
                P.op("pe", lambda e, b=b, w=w, rhs=rhs, kc=kc: e.matmul(psf(b, n), lhsT=w, rhs=rhs, start=(kc == 0), stop=(kc == nk - 1)),
                     reads=[wt] + rhs_tiles, writes=[PSB[b]])
            return b

        def dump(name, sb_ap, tiles):
            if name in dbg_d:
                P.op("sp", lambda e: e.dma_start(out=dbg_d[name], in_=sb_ap), reads=tiles, writes=[T_dbg], dma=D_dbg)

        DVE_CTS = [0, 1, 2, 3]
        PE_CTS = [4, 5, 6, 7]

        def conv_dve(ct):
            y = ycv[:, ct, :]
            P.op("dve", lambda e: e.tensor_scalar(out=y, in0=zT[:, ct, 241:241 + 512],
                                                  scalar1=cvec[:, CV_WDW + ct * 31:CV_WDW + ct * 31 + 1],
                                                  scalar2=cvec[:, CV_BDW + ct:CV_BDW + ct + 1], op0=ALU.mult, op1=ALU.add),
                 reads=[T_z, T_cv], writes=[T_ycv[ct]])
            for k in range(1, 31):
                wk = cvec[:, CV_WDW + ct * 31 + k:CV_WDW + ct * 31 + k + 1]
                zin = zT[:, ct, 241 + k:241 + k + 512]
                P.op("dve", lambda e, wk=wk, zin=zin: e.scalar_tensor_tensor(out=y, in0=zin, scalar=wk, in1=y, op0=ALU.mult, op1=ALU.add),
                     reads=[T_z, T_cv], writes=[T_ycv[ct]])

        def conv_pe(ct):
            b = next_ps()
            ci = PE_CTS.index(ct)
            for k in range(31):
                zin = zT[:, ct, 241 + k:241 + k + 512]
                w, wt = SS.lhs_block(NSLAB, ci * 31 + k)
                P.op("pe", lambda e, b=b, k=k, zin=zin, w=w: e.matmul(psf(b), lhsT=w, rhs=zin, start=(k == 0), stop=(k == 30)),
                     reads=[wt, T_z], writes=[PSB[b]])
            P.op("act", lambda e, b=b: e.activation(out=ycv[:, ct, :], in_=psf(b), func=AF.Identity, bias=cvec[:, CV_BDW + ct:CV_BDW + ct + 1]),
                 reads=[PSB[b], T_cv], writes=[T_ycv[ct]])

        def diag_prologue():
            T_dgw = Tile("dgw")
            nblk = len(PE_CTS) * 31
            for sl in range(4):
                buf = slabs[sl % NSB]
                tl = T_slab[sl % NSB]
                for j in range(32):
                    blk = sl * 32 + j
                    dst = buf[:, j * 128:(j + 1) * 128]
                    if blk < nblk:
                        ct = PE_CTS[blk // 31]
                        k = blk % 31
                        wk = cvec[:, CV_WDW + ct * 31 + k:CV_WDW + ct * 31 + k + 1]
                        if j % 2 == 0:
                            P.op("act", lambda e, dst=dst, wk=wk: e.activation(out=dst, in_=ident[:, :], func=AF.Copy, scale=wk),
                                 reads=[T_const, T_cv], writes=[tl])
                        else:
                            P.op("dve", lambda e, dst=dst, wk=wk: e.tensor_scalar(out=dst, in0=ident[:, :], scalar1=wk, scalar2=None, op0=ALU.mult),
                                 reads=[T_const, T_cv], writes=[tl])
                    else:
                        P.op("dve", lambda e, dst=dst: e.memset(dst, 0.0), writes=[tl])
                P.op("sp", lambda e, sl=sl, buf=buf: e.dma_start(out=wbf[(NSLAB + sl) * 128:(NSLAB + sl + 1) * 128, :], in_=buf[:, :]),
                     reads=[tl], writes=[T_dgw], dma=D_dgw)
            for sl in range(4):
                slab_tile[NSLAB + sl] = T_dgw

        def phase_A(g):
            rc = (g % 2) * 512
            if g == 1:
                P.op("pool", lambda e: e.tensor_copy(out=zT[:, :, 0:512], in_=zT[:, :, 512:1024]), reads=[T_z], writes=[T_z])
                P.op("pool", lambda e: e.tensor_copy(out=uT[:, :, 0:256], in_=uT[:, :, 512:768]), reads=[T_uT], writes=[T_uT])
            if (g, 0) in staged and (g, 1) in staged:
                post_x(staged.pop((g, 0)), 256, CV_GPRE1)
                i2 = early_loads.pop((g, 2)) if (g, 2) in early_loads else load_x(x_ext[4 * g + 2])
                i3 = early_loads.pop((g, 3)) if (g, 3) in early_loads else load_x(x_ext[4 * g + 3])
                u2 = pre_x(i2)
                post_x(staged.pop((g, 1)), 256 + 128, CV_GPRE1)
                u3 = pre_x(i3)
                post_x(u2, 256 + 256, CV_GPRE1)
                post_x(u3, 256 + 384, CV_GPRE1)
            else:
                for t in range(4):
                    if (g, t) not in staged:
                        staged[(g, t)] = pre_x(load_x(x_ext[4 * g + t]))
                    post_x(staged.pop((g, t)), 256 + t * 128, CV_GPRE1)
            rhs_u = lambda kc: uT[:, kc, 256:768]
            for j in range(8):
                ba = proj_lhs(OFF_A, (2 * j) * 16, 16, rhs_u, [T_uT])
                bb = proj_lhs(OFF_A, (2 * j + 1) * 16, 16, rhs_u, [T_uT])
                si = cnt["sig"] % 2
                cnt["sig"] += 1
                P.op("act", lambda e, bb=bb, si=si: e.activation(out=sigt[si][:, :], in_=psf(bb), func=AF.Sigmoid),
                     reads=[PSB[bb]], writes=[T_sig[si]])
                P.op("dve", lambda e, ba=ba, si=si, j=j: e.tensor_tensor(out=zT[:, j, 512:1024], in0=psf(ba), in1=sigt[si][:, :], op=ALU.mult),
                     reads=[PSB[ba], T_sig[si]], writes=[T_z])
            if g >= 1 and not (dbg and dbg_stop[0] == "E"):
                cast2_piece()
            if g >= 1:
                for ct in DVE_CTS:
                    conv_dve(ct)
                for ct in PE_CTS:
                    conv_pe(ct)
            for m in range(8):
                b = proj_lhs(OFF_A, (16 + m) * 16, 16, rhs_u, [T_uT])
                P.op("act", lambda e, b=b, m=m: e.activation(out=qT[:, m, rc:rc + 512], in_=psf(b), func=AF.Copy, scale=QSCALE),
                     reads=[PSB[b]], writes=[T_q[g % 2]])
            for m in range(8):
                b = proj_lhs(OFF_A, (24 + m) * 16, 16, rhs_u, [T_uT])
                P.op("act", lambda e, b=b, m=m: e.activation(out=kT[:, m, rc:rc + 512], in_=psf(b), func=AF.Copy),
                     reads=[PSB[b]], writes=[T_k[g % 2]])
            for nb in range(2):
                for t in range(4):
                    b = 4 + (cnt["pv"] % 2)
                    cnt["pv"] += 1
                    for kc in range(16):
                        w, wt = SS.rhs_block(OFF_V, nb * 16 + kc)
                        P.op("pe", lambda e, b=b, w=w, kc=kc, t=t: e.matmul(psf(b), lhsT=uT[:, kc, 256 + t * 128:256 + (t + 1) * 128], rhs=w,
                                                                          start=(kc == 0), stop=(kc == 15)),
                             reads=[wt, T_uT], writes=[PSB[b]])
                    slot = (4 * g + t) % 8
                    dst = vv[:, slot, nb * 512:(nb + 1) * 512]
                    P.op("act", lambda e, b=b, dst=dst: e.activation(out=dst, in_=psf(b), func=AF.Copy),
                         reads=[PSB[b]], writes=[T_v[slot]])

        state = {"tbv": None}

        def ring_col(e_):
            return ((e_ // 4) % 2) * 512 + (e_ % 4) * 128

        def phase_B(g):
            e0 = 4 * g - 2
            if not (dbg and dbg_stop[0] == "E"):
                cast2_piece()
            items = []
            for e_ in range(e0, e0 + 4):
                p = e_ - 2
                var = {0: 0, 1: 1, 30: 3, 31: 4}.get(p, 2)
                offs = {0: [-2, -1, 0, 1, 2, 3], 4: [-3, -2, -1, 0, 1, 2]}.get(var, [-2, -1, 0, 1, 2])
                for h in range(8):
                    items.append((e_, h, var, offs, cnt["ev"] % 2))
                    cnt["ev"] += 1

            def att_S(it):
                e_, h, var, offs, par = it
                sb_ = par * 2
                qc = ring_col(e_)
                ktiles = sorted(set(T_k[((e_ + o) // 4) % 2] for o in offs), key=lambda t: t.name)
                for s_, o in enumerate(offs):
                    kc_ = ring_col(e_ + o)
                    P.op("pe", lambda e, s_=s_, kc_=kc_: e.matmul(
                        psall[:, sb_ * 512 + s_ * 128:sb_ * 512 + (s_ + 1) * 128],
                        lhsT=kT[:, h, kc_:kc_ + 128], rhs=qT[:, h, qc:qc + 128], start=True, stop=True),
                        reads=ktiles + [T_q[(e_ // 4) % 2]], writes=[PSB[sb_], PSB[sb_ + 1]])

            def att_mid(it):
                e_, h, var, offs, par = it
                sb_ = par * 2
                ns = len(offs)
                if state["tbv"] != var:
                    state["tbv"] = var
                    P.op("sp", lambda e: e.dma_start(out=ar2[:, 0:6144], in_=ebt[var]), reads=[T_ebt], writes=[T_tb], dma=D_tb)
                P.op("act", lambda e: e.activation(out=ets[par][:, 0:ns * 128], in_=psall[:, sb_ * 512:sb_ * 512 + ns * 128], func=AF.Exp),
                     reads=[PSB[sb_], PSB[sb_ + 1]], writes=[T_et[par]])
                P.op("dve", lambda e: e.tensor_tensor(out=ets[par][:, 0:ns * 128], in0=ets[par][:, 0:ns * 128],
                                                      in1=tb[:, h, 0:ns, :].rearrange("p s q -> p (s q)"), op=ALU.mult),
                     reads=[T_et[par], T_tb], writes=[T_et[par]])

            def att_PV(it):
                e_, h, var, offs, par = it
                pb = 4 + par
                ns = len(offs)
                for s_, o in enumerate(offs):
                    slot = (e_ + o) % 8
                    P.op("pe", lambda e, s_=s_, slot=slot: e.matmul(
                        psall[:, pb * 512:pb * 512 + 128], lhsT=vv[:, slot, h * 128:(h + 1) * 128],
                        rhs=ets[par][:, s_ * 128:(s_ + 1) * 128], start=(s_ == 0), stop=(s_ == ns - 1)),
                        reads=[T_v[slot], T_et[par]], writes=[PSB[pb]])
                for s_ in range(ns):
                    P.op("pe", lambda e, s_=s_: e.matmul(
                        psall[:, pb * 512 + 128:pb * 512 + 256], lhsT=ones_b[:, :],
                        rhs=ets[par][:, s_ * 128:(s_ + 1) * 128], start=(s_ == 0), stop=(s_ == ns - 1)),
                        reads=[T_ones, T_et[par]], writes=[PSB[pb]])

            def att_post(it):
                e_, h, var, offs, par = it
                pb = 4 + par
                P.op("dve", lambda e: e.reciprocal(out=rrs[par][:, :], in_=psall[:, pb * 512 + 128:pb * 512 + 256]),
                     reads=[PSB[pb]], writes=[T_rr[par]])
                tcol = (e_ - e0) * 128
                P.op("dve", lambda e: e.tensor_tensor(out=attnT[:, h, tcol:tcol + 128], in0=psall[:, pb * 512:pb * 512 + 128],
                                                      in1=rrs[par][:, :], op=ALU.mult),
                     reads=[PSB[pb], T_rr[par]], writes=[T_attn])

            def ln_stat(ct):
                y = ycv[:, ct, :]
                ti = ct % 2
                P.op("act", lambda e: e.activation(out=tmpf[ti][:, :], in_=y, func=AF.Square),
                     reads=[T_ycv[ct]], writes=[T_tmp[ti]])
                P.op("pe", lambda e: e.matmul(psf(6), lhsT=ones_f[:, :], rhs=y, start=(ct == 0), stop=(ct == 7)),
                     reads=[T_ones, T_ycv[ct]], writes=[PSB[6]])
                P.op("pe", lambda e: e.matmul(psf(7), lhsT=ones_f[:, :], rhs=tmpf[ti][:, :], start=(ct == 0), stop=(ct == 7)),
                     reads=[T_ones, T_tmp[ti]], writes=[PSB[7]])

            def ln_part():
                lnm, lnr = sigt[0], sigt[1]
                T_lnl = [T_sig[0], T_sig[1]]
                P.op("dve", lambda e: e.tensor_scalar(out=lnm[:, :], in0=psf(6), scalar1=1.0 / 1024, scalar2=None, op0=ALU.mult),
                     reads=[PSB[6]], writes=T_lnl)
                P.op("dve", lambda e: e.tensor_tensor(out=lnr[:, :], in0=lnm[:, :], in1=lnm[:, :], op=ALU.mult), reads=T_lnl, writes=T_lnl)
                P.op("dve", lambda e: e.scalar_tensor_tensor(out=lnr[:, :], in0=psf(7), scalar=1.0 / 1024, in1=lnr[:, :], op0=ALU.mult, op1=ALU.subtract),
                     reads=[PSB[7]] + T_lnl, writes=T_lnl)
                rstd_chain(P, lnr[:, :], lnr[:, :], tmpf[0][:, :], [T_sig[0], T_sig[1], T_tmp[0]], 1.0)
                for ct in range(8):
                    y = ycv[:, ct, :]
                    P.op("dve", lambda e, y=y: e.tensor_tensor(out=y, in0=y, in1=lnm[:, :], op=ALU.subtract), reads=T_lnl, writes=[T_ycv[ct]])
                    P.op("dve", lambda e, y=y: e.tensor_tensor(out=y, in0=y, in1=lnr[:, :], op=ALU.mult), reads=T_lnl, writes=[T_ycv[ct]])
                    P.op("act", lambda e, y=y, ct=ct: e.activation(out=cT[:, ct, :], in_=y, func=AF.Silu,
                                                                  scale=cvec[:, CV_LNG + ct:CV_LNG + ct + 1], bias=cvec[:, CV_LNB + ct:CV_LNB + ct + 1]),
                         reads=[T_ycv[ct], T_cv], writes=[T_cT])
                if g + 1 < NGRP:
                    P.op("pool", lambda e: e.tensor_copy(out=zT[:, :, 0:512], in_=zT[:, :, 512:1024]), reads=[T_z], writes=[T_z])

            def ga_proj(m):
                b = 6 + (m % 2)
                for kc in range(16):
                    w, wt = SS.lhs_block(OFF_C, m * 16 + kc)
                    P.op("pe", lambda e, w=w, kc=kc: e.matmul(psf(b), lhsT=w, rhs=uT[:, kc, 0:512], start=(kc == 0), stop=(kc == 15)),
                         reads=[wt, T_uT], writes=[PSB[b]])
                P.op("act", lambda e: e.activation(out=sga[:, m, :], in_=psf(b), func=AF.Sigmoid),
                     reads=[PSB[b]], writes=[T_ycv[m // 2]])

            att_S(items[0])
            for i, it in enumerate(items):
                if i + 1 < len(items):
                    att_S(items[i + 1])
                att_mid(it)
                att_PV(it)
                att_post(it)
                if 8 <= i < 16:
                    ln_stat(i - 8)
                if i == 16:
                    ln_part()
                if i >= 16:
                    ga_proj(i - 16)

        def phase_C(g):
            par = (g - 1) % 2
            if dbg and dbg_stop[0] == "E":
                if g == 1:
                    cast2_piece(len(c2_pieces))
            else:
                cast2_piece()
            if g + 1 < NGRP:
                stage_x(g + 1, [0, 1])
            for m in range(16):
                if m == 8 and not (dbg and dbg_stop[0] == "E"):
                    cast2_piece()
                b0 = 256 + m * 32
                bya = proj_lhs(OFF_C, b0, 8, lambda kc: attnT[:, kc, :], [T_attn])
                P.op("dve", lambda e, bya=bya, m=m: e.tensor_tensor(out=tmpf[0][:, :], in0=psf(bya), in1=sga[:, m, :], op=ALU.mult),
                     reads=[PSB[bya], T_ycv[m // 2]], writes=[T_tmp[0]])
                byc = proj_lhs(OFF_C, b0 + 8, 8, lambda kc: cT[:, kc, :], [T_cT])
                bgb = proj_lhs(OFF_C, b0 + 16, 16, lambda kc: uT[:, kc, 0:512], [T_uT])
                P.op("act", lambda e, bgb=bgb: e.activation(out=sigt[1][:, :], in_=psf(bgb), func=AF.Sigmoid), reads=[PSB[bgb]], writes=[T_sig[1]])
                P.op("dve", lambda e, byc=byc: e.tensor_tensor(out=tmpf[1][:, :], in0=psf(byc), in1=sigt[1][:, :], op=ALU.mult),
                     reads=[PSB[byc], T_sig[1]], writes=[T_tmp[1]])
                P.op("pool", lambda e, m=m: e.tensor_tensor(out=mT(m, par), in0=tmpf[0][:, :], in1=tmpf[1][:, :], op=ALU.add),
                     reads=[T_tmp[0], T_tmp[1]], writes=[T_mTp[par]])

        def phase_D(g):
            e0 = 4 * g - 2
            par = (g - 1) % 2
            if not (dbg and dbg_stop[0] == "E"):
                cast2_piece()
            if g + 1 < NGRP:
                P.op("act", lambda e: e.activation(out=uT[:, :, 0:256], in_=uT[:, :, 512:768], func=AF.Copy), reads=[T_uT], writes=[T_uT])
                for t in (2, 3):
                    early_loads[(g + 1, t)] = load_x(x_ext[4 * (g + 1) + t])
            for nb in range(4):
                for t in range(4):
                    b = 4 + (cnt["pv"] % 2)
                    cnt["pv"] += 1
                    for kc in range(16):
                        w, wt = SS.rhs_block(OFF_O, nb * 16 + kc)
                        P.op("pe", lambda e, b=b, w=w, kc=kc, t=t: e.matmul(psf(b), lhsT=mT(kc, par)[:, t * 128:(t + 1) * 128], rhs=w,
                                                                          start=(kc == 0), stop=(kc == 15)),
                             reads=[wt, T_mTp[par]], writes=[PSB[b]])
                    P.op("act", lambda e, b=b, t=t, nb=nb: e.activation(out=sqj1[:, :], in_=psf(b), func=AF.Square,
                                                                       accum_out=stat[:, t * 8 + 2 + nb:t * 8 + 3 + nb]),
                         reads=[PSB[b]], writes=[T_sqj1, T_stat[t]])
                    P.op("dve", lambda e, b=b, t=t, nb=nb: e.tensor_copy(out=o1[:, t, nb * 512:(nb + 1) * 512], in_=psf(b)),
                         reads=[PSB[b]], writes=[T_o1[t]])
            ssums = [stat[:, t * 8 + 6:t * 8 + 7] for t in range(4)]
            srss = [stat[:, t * 8 + 7:t * 8 + 8] for t in range(4)]
            for t in range(4):
                P.op("dve", lambda e, t=t: e.tensor_scalar(out=sjunk[:, 0:4], in0=stat[:, t * 8 + 2:t * 8 + 6], scalar1=1.0, scalar2=0.0,
                                                          op0=ALU.mult, op1=ALU.add, accum_out=ssums[t]),
                     reads=[T_stat[t]], writes=[T_stat[t]], late=True)
            for t in range(4):
                P.op("dve", lambda e, t=t: e.tensor_scalar(out=srss[t], in0=ssums[t], scalar1=1.0 / D, scalar2=EPS, op0=ALU.mult, op1=ALU.add),
                     reads=[T_stat[t]], writes=[T_stat[t]])
            for t in range(4):
                P.op("act", lambda e, t=t: e.activation(out=sjunk[:, 4 + t:5 + t], in_=srss[t], func=AF.Sqrt), reads=[T_stat[t]], writes=[T_stat[t]])
            for t in range(4):
                P.op("dve", lambda e, t=t: e.reciprocal(out=srss[t], in_=sjunk[:, 4 + t:5 + t]), reads=[T_stat[t]], writes=[T_stat[t]], late=True)
            for t in range(4):
                ptile = e0 + t - 2
                P.op("dve", lambda e, t=t: e.scalar_tensor_tensor(out=o1[:, t, :], in0=o1[:, t, :], scalar=srss[t], in1=gpost[:, :],
                                                                 op0=ALU.mult, op1=ALU.mult),
                     reads=[T_stat[t], T_gp], writes=[T_o1[t]])
                P.op("pool", lambda e, t=t: e.dma_start(out=o1[:, t, :], in_=x_ext[e0 + t], accum_op=ALU.add),
                     reads=[T_o1[t]], writes=[T_o1[t]], dma=D_acc[t])
                P.op("pool", lambda e, t=t, ptile=ptile: e.dma_start(out=hbuf[ptile], in_=o1[:, t, :]),
                     reads=[T_o1[t]], writes=[T_hbuf[ptile]], dma=D_h[t])

        def table_prologue():
            stage_tiles = T_o1 + T_ycv + [T_attn, T_cT]
            for v_ in range(5):
                P.op("sp", lambda e, v_=v_: e.dma_start(out=ar1[:, 0:6144], in_=tabs_d[v_]), writes=stage_tiles, dma=D_tb)
                P.op("act", lambda e: e.activation(out=ar2[:, 0:6144], in_=ar1[:, 0:6144], func=AF.Exp),
                     reads=stage_tiles, writes=[T_tb])
                P.op("sp", lambda e, v_=v_: e.dma_start(out=ebt[v_], in_=ar2[:, 0:6144]), reads=[T_tb], writes=[T_ebt], dma=D_ebt)

        stop = dbg_stop[0] if dbg else None
        table_prologue()
        diag_prologue()
        for g in range(NGRP):
            phase_A(g)
            if dbg and stop == "A" and g == 1:
                dump("dbg_q", qT[:, :, :], T_q)
                dump("dbg_k", kT[:, :, :], T_k)
                dump("dbg_v", vv[:, :, :], T_v)
                dump("dbg_z", zT[:, :, :], [T_z])
                break
            if g >= 1:
                phase_B(g)
                if dbg and stop == "B" and g == 1:
                    dump("dbg_attn", ar1_bf[:, 8192:12288], [T_attn])
                    dump("dbg_c", ar1_bf[:, 12288:16384], [T_cT])
                    break
                phase_C(g)
                if dbg and stop == "C" and g == 1:
                    pass
                    break
                phase_D(g)
                if g == NGRP - 1:
                    cast2_piece(len(c2_pieces))
                if dbg and stop == "D" and g == 1:
                    dump("dbg_h", ar1[:, :], T_o1)
                    dump("dbg_stat", stat[:, :], T_stat)
                    break
                if dbg and stop == "E" and g == dbg_stop[1]:
                    break
        P.emit()

    if dbg and stop in ("A", "B", "C", "D"):
        return nc

    with ExitStack() as es:
        def sb(name, shape, dt):
            return es.enter_context(nc.sbuf_tensor(name, shape, dt))

        def sem(name):
            return es.enter_context(nc.semaphore(name))

        esems = {e: sem("f_" + e) for e in ENGS}
        P = Prog(nc, esems)
        psall = es.enter_context(nc.psum_tensor("psall2", [128, 4096], F32))
        psall_bf = psall.bitcast(BF16)
        PSB = [Tile("psb%d" % i, exclusive=True) for i in range(8)]

        def psf(b, n=512):
            return psall[:, b * 512:b * 512 + n]

        ident = sb("ident2", [128, 128], BF16)
        cvec = sb("cvec2", [128, NCV], F32)
        gpost = sb("gpost2", [128, D], F32)
        uT = sb("u2T", [128, 16, 512], BF16)
        fT = sb("fT", [128, NJ, 512], BF16)
        hh = [sb("hh%d" % i, [128, 4, D], F32) for i in range(2)]
        o2 = sb("o2", [128, 4, D], F32)
        slabs = [sb("slabf%d" % i, [128, 4096], BF16) for i in range(NSB)]
        utms = [sb("utm2_%d" % i, [128, D], BF16) for i in range(2)]
        sqj = sb("sqj", [128, 512], BF16)
        sigt = [sb("sigf%d" % i, [128, 512], F32) for i in range(2)]
        stat = sb("stat2", [128, 64], F32)
        sjunk = sb("sjunk2", [128, 8], F32)

        T_const = Tile("const")
        T_cv = Tile("cvec")
        T_gp = Tile("gpost")
        T_uT = Tile("uT")
        T_fT = [Tile("fT%d" % j) for j in range(NJ)]
        T_h = [[Tile("h%d_%d" % (j, i)) for i in range(4)] for j in range(2)]
        T_o2 = [Tile("o2_%d" % i) for i in range(4)]
        T_slab = [Tile("slab%d" % i) for i in range(NSB)]
        T_utms = [Tile("utm%d" % i) for i in range(2)]
        T_sqj = Tile("sqj")
        T_sig = [Tile("sig%d" % i) for i in range(2)]
        T_stat = [Tile("stat%d" % i) for i in range(8)]
        T_w2 = Tile("wbf_s2")
        D_slab = [P.new_dma_sem(sem("g_slab%d" % i)) for i in range(NSB)]
        D_hl = [P.new_dma_sem(sem("g_h%d" % i)) for i in range(4)]
        D_o = [P.new_dma_sem(sem("g_o%d" % i)) for i in range(4)]
        D_consts = [P.new_dma_sem(sem("g_const%d" % i)) for i in range(3)]

        P.op("pool", lambda e: e.dma_start(out=ident[:, :], in_=ident_d[:, :]), writes=[T_const], dma=D_consts[0])
        P.op("pool", lambda e: e.dma_start(out=cvec[:, :], in_=cvec_d[:, :]), writes=[T_cv], dma=D_consts[1])
        P.op("pool", lambda e: e.dma_start(out=gpost[:, :], in_=gpost_d[1]), writes=[T_gp], dma=D_consts[2])
        SS = SlabStream(P, wbf, slabs, T_slab, D_slab, {})
        cnt = {"pt": 0, "ps": 0, "sig": 0}

        NBLK2 = dbg_stop[1] if (dbg and dbg_stop[0] == "E") else 8

        def load_h(blk):
            hb = blk % 2
            for t in range(4):
                P.op("act", lambda e, t=t: e.dma_start(out=hh[hb][:, t, :], in_=hbuf[4 * blk + t]), writes=[T_h[hb][t]], dma=D_hl[t])

        def norm_T(blk):
            hb = blk % 2
            for t in range(4):
                sq = stat[:, 32 + t * 4:32 + t * 4 + 1]
                rs_ = stat[:, 32 + t * 4 + 1:32 + t * 4 + 2]
                tmp_ = stat[:, 32 + t * 4 + 2:32 + t * 4 + 3]
                ui = t % 2
                utm = utms[ui]
                T_utm = T_utms[ui]
                P.op("act", lambda e, t=t, sq=sq, utm=utm: e.activation(out=utm[:, :], in_=hh[hb][:, t, :], func=AF.Square, accum_out=sq),
                     reads=[T_h[hb][t]], writes=[T_utm, T_stat[4 + t]])
                rstd_chain(P, sq, rs_, tmp_, [T_stat[4 + t]], 1.0 / D)
                P.op("dve", lambda e, t=t, rs_=rs_, utm=utm: e.tensor_scalar(out=utm[:, :], in0=hh[hb][:, t, :], scalar1=rs_, scalar2=None, op0=ALU.mult),
                     reads=[T_h[hb][t], T_stat[4 + t]], writes=[T_utm])
                for half in range(2):
                    b = 6 + half
                    for j in range(8):
                        kc = half * 8 + j
                        P.op("pe", lambda e, b=b, j=j, kc=kc, utm=utm: e.transpose(
                            out=psall_bf[:, b * 1024 + j * 128:b * 1024 + (j + 1) * 128],
                            in_=utm[:, kc * 128:(kc + 1) * 128], identity=ident[:, :]),
                            reads=[T_utm, T_const], writes=[PSB[b]])
                    for j in range(8):
                        kc = half * 8 + j
                        src = psall_bf[:, b * 1024 + j * 128:b * 1024 + (j + 1) * 128]
                        dst = uT[:, kc, t * 128:(t + 1) * 128]
                        gs = cvec[:, CV_GPRE2 + kc:CV_GPRE2 + kc + 1]
                        if half == 0:
                            P.op("act", lambda e, src=src, dst=dst, gs=gs: e.activation(out=dst, in_=src, func=AF.Copy, scale=gs),
                                 reads=[PSB[b], T_cv], writes=[T_uT])
                        else:
                            P.op("dve", lambda e, src=src, dst=dst, gs=gs: e.tensor_scalar(out=dst, in0=src, scalar1=gs, scalar2=None, op0=ALU.mult),
                                 reads=[PSB[b], T_cv], writes=[T_uT])

        load_h(0)
        norm_T(0)
        for blk in range(NBLK2):
            hb = blk % 2
            for j in range(NJ):
                bs = []
                for gu in range(2):
                    b = cnt["ps"] % 6
                    cnt["ps"] += 1
                    for kc in range(16):
                        w, wt = SS.lhs_block(OFF_GU, (2 * j + gu) * 16 + kc)
                        P.op("pe", lambda e, b=b, w=w, kc=kc: e.matmul(psf(b), lhsT=w, rhs=uT[:, kc, :], start=(kc == 0), stop=(kc == 15)),
                             reads=[wt, T_uT], writes=[PSB[b]])
                    bs.append(b)
                si = cnt["sig"] % 2
                cnt["sig"] += 1
                P.op("act", lambda e, b=bs[0], si=si: e.activation(out=sigt[si][:, :], in_=psf(b), func=AF.Silu), reads=[PSB[bs[0]]], writes=[T_sig[si]])
                P.op("dve", lambda e, b=bs[1], si=si, j=j: e.tensor_tensor(out=fT[:, j, :], in0=psf(b), in1=sigt[si][:, :], op=ALU.mult),
                     reads=[PSB[bs[1]], T_sig[si]], writes=[T_fT[j]])
            if blk + 1 < NBLK2:
                load_h(blk + 1)
            for nq in range(4):
                banks = [(nq * 4 + t) % 6 for t in range(4)]
                for kc in range(NJ):
                    w, wt = SS.rhs_block(OFF_D, nq * NJ + kc)
                    for t in range(4):
                        b = banks[t]
                        P.op("pe", lambda e, b=b, w=w, kc=kc, t=t: e.matmul(psf(b), lhsT=fT[:, kc, t * 128:(t + 1) * 128], rhs=w,
                                                                          start=(kc == 0), stop=(kc == NJ - 1)),
                             reads=[wt, T_fT[kc]], writes=[PSB[b]])
                for t in range(4):
                    b = banks[t]
                    P.op("act", lambda e, b=b, t=t, nq=nq: e.activation(out=sqj[:, :], in_=psf(b), func=AF.Square,
                                                                       accum_out=stat[:, t * 8 + 2 + nq:t * 8 + 3 + nq]),
                         reads=[PSB[b]], writes=[T_sqj, T_stat[t]])
                    P.op("dve", lambda e, b=b, t=t, nq=nq: e.tensor_copy(out=o2[:, t, nq * 512:(nq + 1) * 512], in_=psf(b)),
                         reads=[PSB[b]], writes=[T_o2[t]])
                if nq == 1 and blk + 1 < NBLK2:
                    norm_T(blk + 1)
            ssums = [stat[:, t * 8 + 6:t * 8 + 7] for t in range(4)]
            srss = [stat[:, t * 8 + 7:t * 8 + 8] for t in range(4)]
            for t in range(4):
                P.op("dve", lambda e, t=t: e.tensor_scalar(out=sjunk[:, 0:4], in0=stat[:, t * 8 + 2:t * 8 + 6], scalar1=1.0, scalar2=0.0,
                                                          op0=ALU.mult, op1=ALU.add, accum_out=ssums[t]),
                     reads=[T_stat[t]], writes=[T_stat[t]], late=True)
            for t in range(4):
                P.op("dve", lambda e, t=t: e.tensor_scalar(out=srss[t], in0=ssums[t], scalar1=1.0 / D, scalar2=EPS, op0=ALU.mult, op1=ALU.add),
                     reads=[T_stat[t]], writes=[T_stat[t]])
            for t in range(4):
                P.op("act", lambda e, t=t: e.activation(out=sjunk[:, 4 + t:5 + t], in_=srss[t], func=AF.Sqrt), reads=[T_stat[t]], writes=[T_stat[t]])
            for t in range(4):
                P.op("dve", lambda e, t=t: e.reciprocal(out=srss[t], in_=sjunk[:, 4 + t:5 + t]), reads=[T_stat[t]], writes=[T_stat[t]], late=True)
            for t in range(4):
                P.op("dve", lambda e, t=t: e.scalar_tensor_tensor(out=o2[:, t, :], in0=o2[:, t, :], scalar=srss[t], in1=gpost[:, :],
                                                                 op0=ALU.mult, op1=ALU.mult),
                     reads=[T_stat[t], T_gp], writes=[T_o2[t]])
                P.op("dve", lambda e, t=t, hb=hb: e.tensor_tensor(out=o2[:, t, :], in0=o2[:, t, :], in1=hh[hb][:, t, :], op=ALU.add),
                     reads=[T_h[hb][t]], writes=[T_o2[t]])
                P.op("act", lambda e, t=t, blk=blk: e.dma_start(out=out_d[4 * blk + t], in_=o2[:, t, :]),
                     reads=[T_o2[t]], writes=[], dma=D_o[t])
        P.emit()
    return nc


dbg_stop = [None, 1]


def _lhs_blocks(W, cols_tiles):
    K = W.shape[0]
    kc = K // 128
    out = np.empty((len(cols_tiles) * kc, 128, 128), np.float32)
    i = 0
    for c in cols_tiles:
        blk = W[:, c:c + 128].reshape(kc, 128, 128)
        out[i:i + kc] = blk
        i += kc
    return out


def _pack_lhs(blocks):
    nb = blocks.shape[0]
    assert nb % 32 == 0
    return np.ascontiguousarray(blocks.reshape(nb // 32, 32, 128, 128).transpose(0, 2, 1, 3)).reshape(nb // 32 * 128, 4096)


def _rhs_blocks(W, col_starts):
    K = W.shape[0]
    kc = K // 128
    out = np.empty((len(col_starts) * kc, 128, 512), np.float32)
    i = 0
    for c in col_starts:
        out[i:i + kc] = W[:, c:c + 512].reshape(kc, 128, 512)
        i += kc
    return out


def _pack_rhs(blocks):
    nb = blocks.shape[0]
    assert nb % 8 == 0
    return np.ascontiguousarray(blocks.reshape(nb // 8, 8, 128, 512).transpose(0, 2, 1, 3)).reshape(nb // 8 * 128, 4096)


def _build_wall(w_in, w_attn_o, w_conv_o, w_out, w_gate_up, w_down):
    parts = []
    cols = []
    for j in range(8):
        cols += [3072 + 128 * j, 4096 + 128 * j]
    cols += [128 * m for m in range(8)] + [1024 + 128 * m for m in range(8)]
    parts.append(_pack_lhs(_lhs_blocks(w_in, cols)))
    parts.append(_pack_rhs(_rhs_blocks(w_in, [2048, 2560])))
    blks = [_lhs_blocks(w_in, [5120 + 128 * m for m in range(16)])]
    for m in range(16):
        blks.append(_lhs_blocks(w_attn_o, [128 * m]))
        blks.append(_lhs_blocks(w_conv_o, [128 * m]))
        blks.append(_lhs_blocks(w_in, [7168 + 128 * m]))
    parts.append(_pack_lhs(np.concatenate(blks, 0)))
    parts.append(_pack_rhs(_rhs_blocks(w_out, [0, 512, 1024, 1536])))
    cols = []
    for j in range(NJ):
        cols += [128 * j, DFF + 128 * j]
    parts.append(_pack_lhs(_lhs_blocks(w_gate_up, cols)))
    parts.append(_pack_rhs(_rhs_blocks(w_down, [0, 512, 1024, 1536])))
    wall = np.concatenate(parts, 0)
    assert wall.shape == (NSLAB * 128, 4096), wall.shape
    return wall


def _build_tables(rpb, qd):
    NEG = np.float32(-30000.0)
    tabs = np.full((5, 128, 8, 6, 128), NEG, np.float32)
    kk = np.arange(128)
    kr2, kcol = kk // 64, kk % 64
    qq = np.arange(128)
    qr2, qcol = qq // 64, qq % 64
    for vi, p in enumerate([0, 1, 2, 30, 31]):
        offs = {0: [-2, -1, 0, 1, 2, 3], 4: [-3, -2, -1, 0, 1, 2]}.get(vi, [-2, -1, 0, 1, 2])
        r0 = 64 * qd + 2 * p
        for s, o in enumerate(offs):
            krow = (r0 + 2 * o + kr2)[:, None]
            qrow = (r0 + qr2)[None, :]
            rs = np.clip(qrow - 4, 0, 248)
            cs = np.clip(qcol - 8, 0, 48)[None, :]
            kc_ = kcol[:, None]
            ok = (krow >= rs) & (krow < rs + 8) & (kc_ >= cs) & (kc_ < cs + 16) & (krow >= 0) & (krow < 256)
            dr = np.clip(krow - qrow + 7, 0, 14)
            dc = np.clip(kc_ - qcol[None, :] + 15, 0, 30)
            for h in range(8):
                vals = rpb[h][dr, dc]
                tabs[vi, :, h, s, :] = np.where(ok, vals, NEG)
    return tabs.reshape(5, 128, 6144)


def _build_cvec(mix_pre_g, ffn_pre_g, w_dw, b_dw, ln_g, ln_b):
    cv = np.zeros((128, NCV), np.float32)
    cv[:, CV_GPRE1:CV_GPRE1 + 16] = mix_pre_g.reshape(16, 128).T
    cv[:, CV_GPRE2:CV_GPRE2 + 16] = ffn_pre_g.reshape(16, 128).T
    cv[:, CV_WDW:CV_WDW + 248] = w_dw.reshape(31, 8, 128).transpose(2, 1, 0).reshape(128, 248)
    cv[:, CV_BDW:CV_BDW + 8] = b_dw.reshape(8, 128).T
    cv[:, CV_LNG:CV_LNG + 8] = ln_g.reshape(8, 128).T
    cv[:, CV_LNB:CV_LNB + 8] = ln_b.reshape(8, 128).T
    return cv


def _prep_inputs(x, mix_pre_g, mix_post_g, w_in, rpb, w_attn_o, w_dw, b_dw, conv_ln_g, conv_ln_b,
                 w_conv_o, w_out, ffn_pre_g, ffn_post_g, w_gate_up, w_down):
    f = lambda a: np.asarray(a, np.float32)
    x = f(x)
    wall = _build_wall(f(w_in)[0], f(w_attn_o)[0], f(w_conv_o)[0], f(w_out)[0], f(w_gate_up)[0], f(w_down)[0])
    cv = _build_cvec(f(mix_pre_g)[0], f(ffn_pre_g)[0], f(w_dw)[0], f(b_dw)[0], f(conv_ln_g)[0], f(conv_ln_b)[0])
    gp = np.stack([np.broadcast_to(f(mix_post_g)[0], (128, D)), np.broadcast_to(f(ffn_post_g)[0], (128, D))]).astype(np.float32)
    gp = np.ascontiguousarray(gp)
    ident = np.eye(128, dtype=np.float32)
    tabs_q = [_build_tables(f(rpb)[0], qd) for qd in range(4)]
    in_maps = []
    for c in range(NCORES):
        b, qd = c // 4, c % 4
        xe = np.zeros((NT_EXT * 128, D), np.float32)
        lo = 4096 * qd - 256
        hi = 4096 * qd + 4096 + 256
        slo, shi = max(lo, 0), min(hi, 16384)
        xe[slo - lo:shi - lo] = x[b, slo:shi]
        in_maps.append({"x_ext": xe.reshape(NT_EXT, 128, D), "wall": wall, "cvec": cv, "gpost": gp,
                        "tabs": tabs_q[qd], "ident": ident})
    return in_maps


def kernel(**inputs):
    in_maps = _prep_inputs(**inputs)
    nc = build_program()
    res = run_bass_kernel_spmd(nc, in_maps, core_ids=list(range(NCORES)))
    out = np.empty((2, 16384, D), np.float32)
    for c in range(NCORES):
        b, qd = c // 4, c % 4
        out[b, 4096 * qd:4096 * (qd + 1)] = np.asarray(res.results[c]["out"]).reshape(4096, D)
    return out
```

```python
import numpy as np
import concourse.bass as bass
import concourse.mybir as mybir
from concourse.bass_utils import run_bass_kernel_spmd

F32 = mybir.dt.float32
BF16 = mybir.dt.bfloat16
AF = mybir.ActivationFunctionType
ALU = mybir.AluOpType
AX = mybir.AxisListType

D = 2048
NCORES = 8
TOK_CORE = 4096
NT_EXT = 36
NGRP = 9
DFF = 5632
NJ = 44
EPS = 1e-6
QSCALE = 128 ** -0.5
S_A, S_V, S_C, S_O, S_GU, S_D = 16, 4, 24, 8, 44, 22
OFF_A = 0
OFF_V = OFF_A + S_A
OFF_C = OFF_V + S_V
OFF_O = OFF_C + S_C
OFF_GU = OFF_O + S_O
OFF_D = OFF_GU + S_GU
NSLAB = OFF_D + S_D
NS1 = OFF_GU
NCV = 16 + 16 + 8 * 31 + 24
CV_GPRE1, CV_GPRE2, CV_WDW, CV_BDW, CV_LNG, CV_LNB = 0, 16, 32, 32 + 248, 32 + 256, 32 + 264
NSB = 3


class Tile:
    def __init__(self, name, exclusive=False):
        self.name = name
        self.exclusive = exclusive
        self.writer = None
        self.readers = []
        self.aliases = []

    def group(self):
        return [self] + self.aliases


def alias(a, b):
    a.aliases.append(b)
    b.aliases.append(a)


class DmaSem:
    def __init__(self, sem):
        self.sem = sem
        self.count = 0


class Op:
    __slots__ = ("eng", "fn", "deps", "dma", "semval", "needs_inc", "idx", "late")

    def __init__(self, eng, fn, dma):
        self.eng = eng
        self.fn = fn
        self.dma = dma
        self.deps = []
        self.semval = None
        self.needs_inc = False
        self.idx = -1
        self.late = False


ENGS = ["pe", "act", "dve", "pool", "sp"]


class Prog:
    def __init__(self, nc, esems):
        self.nc = nc
        self.ops = {e: [] for e in ENGS}
        self.esem = esems
        self.dmasems = []

    def new_dma_sem(self, sem):
        d = DmaSem(sem)
        self.dmasems.append(d)
        return d

    def op(self, eng, fn, reads=(), writes=(), dma=None, late=False):
        o = Op(eng, fn, dma)
        o.late = late
        writes = list(writes) + [t for t in reads if t.exclusive]
        reads = [t for t in reads if not t.exclusive]
        raw = []
        for t in reads:
            for tt in t.group():
                if tt.writer is not None:
                    raw.append(tt.writer)
        for t in writes:
            for tt in t.group():
                if tt.writer is not None:
                    raw.append(tt.writer)
                raw.extend(tt.readers)
        best = {}
        deps = []
        seen = set()
        for d in raw:
            if id(d) in seen:
                continue
            seen.add(id(d))
            if d.dma is not None:
                deps.append(d)
            else:
                if d.eng == eng and dma is None and not d.late:
                    continue
                b = best.get(d.eng)
                if b is None or d.idx > b.idx:
                    best[d.eng] = d
        deps.extend(best.values())
        o.deps = deps
        for d in deps:
            if d.dma is None:
                d.needs_inc = True
        for t in reads:
            t.readers.append(o)
        for t in writes:
            t.writer = o
            t.readers = []
        if dma is not None:
            dma.count += 16
            o.semval = dma.count
        o.idx = len(self.ops[eng])
        self.ops[eng].append(o)
        return o

    def emit(self):
        nc = self.nc
        for e in ENGS:
            c = 0
            for o in self.ops[e]:
                if o.dma is None and o.needs_inc:
                    c += 1
                    o.semval = c
            assert c < 60000, (e, c)
        for d in self.dmasems:
            assert d.count < 60000

        def run(ename, eng):
            known = {}
            for o in self.ops[ename]:
                need = {}
                for d in o.deps:
                    s = d.dma.sem if d.dma is not None else self.esem[d.eng]
                    k = id(s)
                    if k not in need or need[k][1] < d.semval:
                        need[k] = (s, d.semval)
                for k, (s, v) in need.items():
                    if known.get(k, 0) >= v:
                        continue
                    eng.wait_ge(s, v)
                    known[k] = v
                ins = o.fn(eng)
                if o.dma is not None:
                    ins.then_inc(o.dma.sem, 16)
                elif o.needs_inc:
                    ins.then_inc(self.esem[ename], 1)
            if ename == "sp":
                for d in self.dmasems:
                    if d.count > 0:
                        eng.wait_ge(d.sem, d.count)

        with nc.Block() as block:
            @block.tensor
            def _(e):
                run("pe", e)

            @block.scalar
            def _(e):
                run("act", e)

            @block.vector
            def _(e):
                run("dve", e)

            @block.gpsimd
            def _(e):
                run("pool", e)

            @block.sync
            def _(e):
                run("sp", e)


class SlabStream:
    def __init__(self, P, wbf, bufs, buf_tiles, dsems, src_tile):
        self.P = P
        self.wbf = wbf
        self.bufs = bufs
        self.tiles = buf_tiles
        self.dsems = dsems
        self.src_tile = src_tile
        self.loaded = {}
        self.counter = 0
        self.resident = [None] * len(bufs)

    def get(self, slab):
        if slab in self.loaded:
            return self.loaded[slab]
        i = self.counter % len(self.bufs)
        self.counter += 1
        old = self.resident[i]
        if old is not None:
            del self.loaded[old]
        self.resident[i] = slab
        self.loaded[slab] = i
        buf = self.bufs[i]
        src = self.wbf[slab * 128:(slab + 1) * 128, :]
        self.P.op("sp", lambda e, buf=buf, src=src: e.dma_start(out=buf[:, :], in_=src),
                  reads=([self.src_tile[slab]] if slab in self.src_tile else []), writes=[self.tiles[i]], dma=self.dsems[i])
        return i

    def lhs_block(self, base, b):
        i = self.get(base + b // 32)
        o = (b % 32) * 128
        return self.bufs[i][:, o:o + 128], self.tiles[i]

    def rhs_block(self, base, b):
        i = self.get(base + b // 8)
        o = (b % 8) * 512
        return self.bufs[i][:, o:o + 512], self.tiles[i]


def rstd_chain(P, src, dst, tmp, tiles, scale):
    P.op("dve", lambda e: e.tensor_scalar(out=dst, in0=src, scalar1=scale, scalar2=EPS, op0=ALU.mult, op1=ALU.add),
         reads=tiles, writes=tiles)
    P.op("act", lambda e: e.activation(out=tmp, in_=dst, func=AF.Sqrt), reads=tiles, writes=tiles)
    P.op("dve", lambda e: e.reciprocal(out=dst, in_=tmp), reads=tiles, writes=tiles, late=True)

def build_program(dbg=None):
    nc = bass.Bass("TRN2", target_bir_lowering=False)
    x_ext = nc.dram_tensor("x_ext", [NT_EXT, 128, D], F32, kind="ExternalInput").ap()
    wall = nc.dram_tensor("wall", [NSLAB * 128, 4096], F32, kind="ExternalInput").ap()
    cvec_d = nc.dram_tensor("cvec", [128, NCV], F32, kind="ExternalInput").ap()
    gpost_d = nc.dram_tensor("gpost", [2, 128, D], F32, kind="ExternalInput").ap()
    tabs_d = nc.dram_tensor("tabs", [5, 128, 6144], F32, kind="ExternalInput").ap()
    ident_d = nc.dram_tensor("ident", [128, 128], F32, kind="ExternalInput").ap()
    out_d = nc.dram_tensor("out", [32, 128, D], F32, kind="ExternalOutput").ap()
    wbf = nc.dram_tensor("wbf", [(NSLAB + 4) * 128, 4096], BF16, kind="Internal").ap()
    ebt = nc.dram_tensor("ebt", [5, 128, 6144], BF16, kind="Internal").ap()
    hbuf = nc.dram_tensor("hbuf", [32, 128, D], F32, kind="Internal").ap()
    dbg_d = {}
    if dbg:
        for name, shape, dt in dbg:
            dbg_d[name] = nc.dram_tensor(name, shape, dt, kind="ExternalOutput").ap()

    from contextlib import ExitStack

    with ExitStack() as es:
        def sb(name, shape, dt):
            return es.enter_context(nc.sbuf_tensor(name, shape, dt))

        def sem(name):
            return es.enter_context(nc.semaphore(name))

        esems = {e: sem("e_" + e) for e in ENGS}
        P = Prog(nc, esems)

        psall = es.enter_context(nc.psum_tensor("psall", [128, 4096], F32))
        psall_bf = psall.bitcast(BF16)
        PSB = [Tile("psb%d" % i, exclusive=True) for i in range(8)]

        def psf(b, n=512, nb=1):
            return psall[:, b * 512:b * 512 + n]

        ident = sb("ident_sb", [128, 128], BF16)
        ones_f = sb("ones_f", [128, 128], F32)
        ones_b = sb("ones_b", [128, 128], BF16)
        cvec = sb("cvec_sb", [128, NCV], F32)
        gpost = sb("gpost_sb", [128, D], F32)
        uT = sb("uT", [128, 16, 768], BF16)
        qT = sb("qT", [128, 8, 1024], BF16)
        kT = sb("kT", [128, 8, 1024], BF16)
        vv = sb("vv", [128, 8, 1024], BF16)
        zT = sb("zT", [128, 8, 1024], BF16)
        slabs = [sb("slab%d" % i, [128, 4096], BF16) for i in range(NSB)]
        xs = [sb("xs%d" % i, [128, D], F32) for i in range(2)]
        utms = [sb("utm%d" % i, [128, D], BF16) for i in range(2)]
        utm = utms[0]
        ar1 = sb("ar1", [128, 8192], F32)
        ar2 = sb("ar2", [128, 8192], BF16)
        sigt = [sb("sigt%d" % i, [128, 512], F32) for i in range(2)]
        tmpf = [sb("tmpf%d" % i, [128, 512], F32) for i in range(2)]
        stat = sb("stat", [128, 64], F32)
        sjunk = sb("sjunk", [128, 8], F32)
        rrs = [sb("rr%d" % i, [128, 128], F32) for i in range(2)]
        sqj1 = sb("sqj1", [128, 512], BF16)

        ycv = ar1[:, 0:4096].rearrange("p (c t) -> p c t", c=8)
        ar1_bf = ar1.bitcast(BF16)
        attnT = ar1_bf[:, 8192:12288].rearrange("p (c t) -> p c t", c=8)
        cT = ar1_bf[:, 12288:16384].rearrange("p (c t) -> p c t", c=8)
        o1 = ar1[:, :].rearrange("p (t d) -> p t d", t=4)
        tb = ar2[:, 0:6144].rearrange("p (h s q) -> p h s q", h=8, s=6)
        ets = [ar2[:, 6144 + i * 768:6144 + (i + 1) * 768] for i in range(2)]

        def mT(m, par):
            return qT[:, m, par * 512:(par + 1) * 512] if m < 8 else kT[:, m - 8, par * 512:(par + 1) * 512]

        T_const = Tile("const")
        T_uT = Tile("uT")
        T_q = [Tile("q%d" % i) for i in range(2)]
        T_k = [Tile("k%d" % i) for i in range(2)]
        T_v = [Tile("v%d" % i) for i in range(8)]
        T_z = Tile("z")
        T_slab = [Tile("slab%d" % i) for i in range(NSB)]
        T_xs = [Tile("xs%d" % i) for i in range(2)]
        T_utms = [Tile("utm%d" % i) for i in range(2)]
        T_utm = T_utms[0]
        T_ycv = [Tile("ycv%d" % i) for i in range(8)]
        T_attn = Tile("attnT")
        T_cT = Tile("cT")
        T_o1 = [Tile("o1_%d" % i) for i in range(4)]
        for t in T_o1:
            for u in T_ycv + [T_attn, T_cT]:
                alias(t, u)
        T_tb = Tile("tb")
        T_et = [Tile("et%d" % i) for i in range(2)]
        T_mTp = [Tile("mergedT%d" % i) for i in range(2)]
        for i in range(2):
            alias(T_mTp[i], T_q[i])
            alias(T_mTp[i], T_k[i])
        T_sig = [Tile("sig%d" % i) for i in range(2)]
        T_tmp = [Tile("tmp%d" % i) for i in range(2)]
        T_stat = [Tile("stat%d" % i) for i in range(8)]
        T_rr = [Tile("rr%d" % i) for i in range(2)]
        T_dg = [Tile("dg%d" % i) for i in range(4)]
        T_sqj1 = Tile("sqj1")
        T_w1 = Tile("wbf_s1")
        T_w2 = Tile("wbf_s2")
        T_ebt = Tile("ebt")
        T_hbuf = [Tile("hbuf%d" % i) for i in range(32)]
        T_dbg = Tile("dbg")

        D_slab = [P.new_dma_sem(sem("d_slab%d" % i)) for i in range(NSB)]
        D_xs = [P.new_dma_sem(sem("d_xs%d" % i)) for i in range(2)]
        D_c1 = [P.new_dma_sem(sem("d_cast%d" % i)) for i in range(4)]
        D_consts = [P.new_dma_sem(sem("d_const%d" % i)) for i in range(3)]
        D_tb = P.new_dma_sem(sem("d_tb"))
        D_dgw = P.new_dma_sem(sem("d_dgw"))
        D_ebt = P.new_dma_sem(sem("d_ebt"))
        D_h = [P.new_dma_sem(sem("d_h%d" % i)) for i in range(4)]
        D_acc = [P.new_dma_sem(sem("d_acc%d" % i)) for i in range(4)]
        early_loads = {}
        D_dbg = P.new_dma_sem(sem("d_dbg"))

        slab_tile = {}

        def cast(bounds, dsem):
            for k in range(len(bounds) - 1):
                s_, e_ = bounds[k], bounds[k + 1]
                tt = Tile("cast%d" % s_)
                P.op("pool", lambda e, s_=s_, e_=e_: e.dma_start(out=wbf[s_ * 128:e_ * 128, :], in_=wall[s_ * 128:e_ * 128, :]),
                     writes=[tt], dma=dsem[k])
                for j in range(s_, e_):
                    slab_tile[j] = tt
        cast([0, 8, 20, 36, NS1], D_c1)
        P.op("pool", lambda e: e.dma_start(out=ident[:, :], in_=ident_d[:, :]), writes=[T_const], dma=D_consts[0])
        T_cv = Tile("cvec")
        P.op("pool", lambda e: e.dma_start(out=cvec[:, :], in_=cvec_d[:, :]), writes=[T_cv], dma=D_consts[1])
        T_gp = Tile("gpost")
        P.op("pool", lambda e: e.dma_start(out=gpost[:, :], in_=gpost_d[0]), writes=[T_gp], dma=D_consts[2])
        T_ones = Tile("ones")
        P.op("dve", lambda e: e.memset(ones_f[:, :], 1.0), writes=[T_ones])
        P.op("dve", lambda e: e.memset(ones_b[:, :], 1.0), writes=[T_ones])
        P.op("pool", lambda e: e.memset(zT[:, :, :], 0.0), writes=[T_z])
        D_c2 = [P.new_dma_sem(sem("d_cast2_%d" % i)) for i in range(5)]
        c2_bounds = [NS1, NS1 + 13, NS1 + 26, NS1 + 39, NS1 + 52, NSLAB]

        def cast2(k):
            s_, e_ = c2_bounds[k], c2_bounds[k + 1]
            P.op("pool", lambda e: e.dma_start(out=wbf[s_ * 128:e_ * 128, :], in_=wall[s_ * 128:e_ * 128, :]),
                 writes=[Tile("cast2_%d" % k)], dma=D_c2[k])
        SS = SlabStream(P, wbf, slabs, T_slab, D_slab, slab_tile)

        cnt = {"xs": 0, "pt": 0, "ps": 0, "sig": 0, "stat": 0, "pv": 0, "ev": 0, "ctmp": 0, "dg": 0}

        def load_x(src_ap):
            i = cnt["xs"] % 2
            cnt["xs"] += 1
            P.op("act", lambda e: e.dma_start(out=xs[i][:, :], in_=src_ap), writes=[T_xs[i]], dma=D_xs[i])
            return i

        def pre_x(i):
            ui = cnt["stat"] % 2
            utm = utms[ui]
            T_utm = T_utms[ui]
            si = cnt["stat"] % 8
            cnt["stat"] += 1
            sq = stat[:, si * 8:si * 8 + 1]
            rs_ = stat[:, si * 8 + 1:si * 8 + 2]
            P.op("act", lambda e: e.activation(out=utm[:, :], in_=xs[i][:, :], func=AF.Square, accum_out=sq),
                 reads=[T_xs[i]], writes=[T_utm, T_stat[si]])
            rstd_chain(P, sq, rs_, stat[:, si * 8 + 2:si * 8 + 3], [T_stat[si]], 1.0 / D)
            P.op("dve", lambda e: e.tensor_scalar(out=utm[:, :], in0=xs[i][:, :], scalar1=rs_, scalar2=None, op0=ALU.mult),
                 reads=[T_xs[i], T_stat[si]], writes=[T_utm])
            return ui

        def post_x(ui, col, gcol0):
            utm = utms[ui]
            T_utm = T_utms[ui]
            for half in range(2):
                b = 6 + half
                for j in range(8):
                    kc = half * 8 + j
                    P.op("pe", lambda e, b=b, j=j, kc=kc: e.transpose(
                        out=psall_bf[:, b * 1024 + j * 128:b * 1024 + (j + 1) * 128],
                        in_=utm[:, kc * 128:(kc + 1) * 128], identity=ident[:, :]),
                        reads=[T_utm, T_const], writes=[PSB[b]])
                for j in range(8):
                    kc = half * 8 + j
                    src = psall_bf[:, b * 1024 + j * 128:b * 1024 + (j + 1) * 128]
                    dst = uT[:, kc, col:col + 128]
                    gs = cvec[:, gcol0 + kc:gcol0 + kc + 1]
                    if half == 0:
                        P.op("act", lambda e, src=src, dst=dst, gs=gs: e.activation(out=dst, in_=src, func=AF.Copy, scale=gs),
                             reads=[PSB[b], T_cv], writes=[T_uT])
                    else:
                        P.op("dve", lambda e, src=src, dst=dst, gs=gs: e.tensor_scalar(out=dst, in0=src, scalar1=gs, scalar2=None, op0=ALU.mult),
                             reads=[PSB[b], T_cv], writes=[T_uT])

        staged = {}

        def stage_x(g, tiles):
            for t in tiles:
                i = load_x(x_ext[4 * g + t])
                staged[(g, t)] = pre_x(i)

        def next_ps():
            b = cnt["ps"] % 4
            cnt["ps"] += 1
            return b

        def proj_lhs(base, b0, nk, rhs_fn, rhs_tiles, n=512):
            b = next_ps()
            for kc in range(nk):
                w, wt = SS.lhs_block(base, b0 + kc)
                rhs = rhs_fn(kc)
                P.op("pe", lambda e, b=b, w=w, rhs=rhs, kc=kc: e.matmul(psf(b, n), lhsT=w, rhs=rhs, start=(kc == 0), stop=(kc == nk - 1)),
                     reads=[wt] + rhs_tiles, writes=[PSB[b]])
            return b

        def dump(name, sb_ap, tiles):
            if name in dbg_d:
                P.op("sp", lambda e: e.dma_start(out=dbg_d[name], in_=sb_ap), reads=tiles, writes=[T_dbg], dma=D_dbg)

        DVE_CTS = [0, 1, 2, 3]
        PE_CTS = [4, 5, 6, 7]

        def conv_dve(ct):
            y = ycv[:, ct, :]
            P.op("dve", lambda e: e.tensor_scalar(out=y, in0=zT[:, ct, 241:241 + 512],
                                                  scalar1=cvec[:, CV_WDW + ct * 31:CV_WDW + ct * 31 + 1],
                                                  scalar2=cvec[:, CV_BDW + ct:CV_BDW + ct + 1], op0=ALU.mult, op1=ALU.add),
                 reads=[T_z, T_cv], writes=[T_ycv[ct]])
            for k in range(1, 31):
                wk = cvec[:, CV_WDW + ct * 31 + k:CV_WDW + ct * 31 + k + 1]
                zin = zT[:, ct, 241 + k:241 + k + 512]
                P.op("dve", lambda e, wk=wk, zin=zin: e.scalar_tensor_tensor(out=y, in0=zin, scalar=wk, in1=y, op0=ALU.mult, op1=ALU.add),
                     reads=[T_z, T_cv], writes=[T_ycv[ct]])

        def conv_pe(ct):
            b = next_ps()
            ci = PE_CTS.index(ct)
            for k in range(31):
                zin = zT[:, ct, 241 + k:241 + k + 512]
                w, wt = SS.lhs_block(NSLAB, ci * 31 + k)
                P.op("pe", lambda e, b=b, k=k, zin=zin, w=w: e.matmul(psf(b), lhsT=w, rhs=zin, start=(k == 0), stop=(k == 30)),
                     reads=[wt, T_z], writes=[PSB[b]])
            P.op("act", lambda e, b=b: e.activation(out=ycv[:, ct, :], in_=psf(b), func=AF.Identity, bias=cvec[:, CV_BDW + ct:CV_BDW + ct + 1]),
                 reads=[PSB[b], T_cv], writes=[T_ycv[ct]])

        def diag_prologue():
            T_dgw = Tile("dgw")
            nblk = len(PE_CTS) * 31
            for sl in range(4):
                buf = slabs[sl % NSB]
                tl = T_slab[sl % NSB]
                for j in range(32):
                    blk = sl * 32 + j
                    dst = buf[:, j * 128:(j + 1) * 128]
                    if blk < nblk:
                        ct = PE_CTS[blk // 31]
                        k = blk % 31
                        wk = cvec[:, CV_WDW + ct * 31 + k:CV_WDW + ct * 31 + k + 1]
                        if j % 2 == 0:
                            P.op("act", lambda e, dst=dst, wk=wk: e.activation(out=dst, in_=ident[:, :], func=AF.Copy, scale=wk),
                                 reads=[T_const, T_cv], writes=[tl])
                        else:
                            P.op("dve", lambda e, dst=dst, wk=wk: e.tensor_scalar(out=dst, in0=ident[:, :], scalar1=wk, scalar2=None, op0=ALU.mult),
                                 reads=[T_const, T_cv], writes=[tl])
                    else:
                        P.op("dve", lambda e, dst=dst: e.memset(dst, 0.0), writes=[tl])
                P.op("sp", lambda e, sl=sl, buf=buf: e.dma_start(out=wbf[(NSLAB + sl) * 128:(NSLAB + sl + 1) * 128, :], in_=buf[:, :]),
                     reads=[tl], writes=[T_dgw], dma=D_dgw)
            for sl in range(4):
                slab_tile[NSLAB + sl] = T_dgw

        def phase_A(g):
            rc = (g % 2) * 512
            if g == 1:
                P.op("pool", lambda e: e.tensor_copy(out=zT[:, :, 0:512], in_=zT[:, :, 512:1024]), reads=[T_z], writes=[T_z])
                P.op("pool", lambda e: e.tensor_copy(out=uT[:, :, 0:256], in_=uT[:, :, 512:768]), reads=[T_uT], writes=[T_uT])
            if (g, 0) in staged and (g, 1) in staged:
                post_x(staged.pop((g, 0)), 256, CV_GPRE1)
                i2 = early_loads.pop((g, 2)) if (g, 2) in early_loads else load_x(x_ext[4 * g + 2])
                i3 = early_loads.pop((g, 3)) if (g, 3) in early_loads else load_x(x_ext[4 * g + 3])
                u2 = pre_x(i2)
                post_x(staged.pop((g, 1)), 256 + 128, CV_GPRE1)
                u3 = pre_x(i3)
                post_x(u2, 256 + 256, CV_GPRE1)
                post_x(u3, 256 + 384, CV_GPRE1)
            else:
                for t in range(4):
                    if (g, t) not in staged:
                        staged[(g, t)] = pre_x(load_x(x_ext[4 * g + t]))
                    post_x(staged.pop((g, t)), 256 + t * 128, CV_GPRE1)
            rhs_u = lambda kc: uT[:, kc, 256:768]
            for j in range(8):
                ba = proj_lhs(OFF_A, (2 * j) * 16, 16, rhs_u, [T_uT])
                bb = proj_lhs(OFF_A, (2 * j + 1) * 16, 16, rhs_u, [T_uT])
                si = cnt["sig"] % 2
                cnt["sig"] += 1
                P.op("act", lambda e, bb=bb, si=si: e.activation(out=sigt[si][:, :], in_=psf(bb), func=AF.Sigmoid),
                     reads=[PSB[bb]], writes=[T_sig[si]])
                P.op("dve", lambda e, ba=ba, si=si, j=j: e.tensor_tensor(out=zT[:, j, 512:1024], in0=psf(ba), in1=sigt[si][:, :], op=ALU.mult),
                     reads=[PSB[ba], T_sig[si]], writes=[T_z])
            if g >= 1:
                for ct in DVE_CTS:
                    conv_dve(ct)
                for ct in PE_CTS:
                    conv_pe(ct)
            for m in range(8):
                b = proj_lhs(OFF_A, (16 + m) * 16, 16, rhs_u, [T_uT])
                P.op("act", lambda e, b=b, m=m: e.activation(out=qT[:, m, rc:rc + 512], in_=psf(b), func=AF.Copy, scale=QSCALE),
                     reads=[PSB[b]], writes=[T_q[g % 2]])
            for m in range(8):
                b = proj_lhs(OFF_A, (24 + m) * 16, 16, rhs_u, [T_uT])
                P.op("act", lambda e, b=b, m=m: e.activation(out=kT[:, m, rc:rc + 512], in_=psf(b), func=AF.Copy),
                     reads=[PSB[b]], writes=[T_k[g % 2]])
            for nb in range(2):
                for t in range(4):
                    b = 4 + (cnt["pv"] % 2)
                    cnt["pv"] += 1
                    for kc in range(16):
                        w, wt = SS.rhs_block(OFF_V, nb * 16 + kc)
                        P.op("pe", lambda e, b=b, w=w, kc=kc, t=t: e.matmul(psf(b), lhsT=uT[:, kc, 256 + t * 128:256 + (t + 1) * 128], rhs=w,
                                                                          start=(kc == 0), stop=(kc == 15)),
                             reads=[wt, T_uT], writes=[PSB[b]])
                    slot = (4 * g + t) % 8
                    dst = vv[:, slot, nb * 512:(nb + 1) * 512]
                    P.op("act", lambda e, b=b, dst=dst: e.activation(out=dst, in_=psf(b), func=AF.Copy),
                         reads=[PSB[b]], writes=[T_v[slot]])

        state = {"tbv": None}

        def ring_col(e_):
            return ((e_ // 4) % 2) * 512 + (e_ % 4) * 128

        def phase_B(g):
            e0 = 4 * g - 2
            items = []
            for e_ in range(e0, e0 + 4):
                p = e_ - 2
                var = {0: 0, 1: 1, 30: 3, 31: 4}.get(p, 2)
                offs = {0: [-2, -1, 0, 1, 2, 3], 4: [-3, -2, -1, 0, 1, 2]}.get(var, [-2, -1, 0, 1, 2])
                for h in range(8):
                    items.append((e_, h, var, offs, cnt["ev"] % 2))
                    cnt["ev"] += 1

            def att_S(it):
                e_, h, var, offs, par = it
                sb_ = par * 2
                qc = ring_col(e_)
                ktiles = sorted(set(T_k[((e_ + o) // 4) % 2] for o in offs), key=lambda t: t.name)
                for s_, o in enumerate(offs):
                    kc_ = ring_col(e_ + o)
                    P.op("pe", lambda e, s_=s_, kc_=kc_: e.matmul(
                        psall[:, sb_ * 512 + s_ * 128:sb_ * 512 + (s_ + 1) * 128],
                        lhsT=kT[:, h, kc_:kc_ + 128], rhs=qT[:, h, qc:qc + 128], start=True, stop=True),
                        reads=ktiles + [T_q[(e_ // 4) % 2]], writes=[PSB[sb_], PSB[sb_ + 1]])

            def att_mid(it):
                e_, h, var, offs, par = it
                sb_ = par * 2
                ns = len(offs)
                if state["tbv"] != var:
                    state["tbv"] = var
                    P.op("sp", lambda e: e.dma_start(out=ar2[:, 0:6144], in_=ebt[var]), reads=[T_ebt], writes=[T_tb], dma=D_tb)
                P.op("act", lambda e: e.activation(out=ets[par][:, 0:ns * 128], in_=psall[:, sb_ * 512:sb_ * 512 + ns * 128], func=AF.Exp),
                     reads=[PSB[sb_], PSB[sb_ + 1]], writes=[T_et[par]])
                P.op("dve", lambda e: e.tensor_tensor(out=ets[par][:, 0:ns * 128], in0=ets[par][:, 0:ns * 128],
                                                      in1=tb[:, h, 0:ns, :].rearrange("p s q -> p (s q)"), op=ALU.mult),
                     reads=[T_et[par], T_tb], writes=[T_et[par]])

            def att_PV(it):
                e_, h, var, offs, par = it
                pb = 4 + par
                ns = len(offs)
                for s_, o in enumerate(offs):
                    slot = (e_ + o) % 8
                    P.op("pe", lambda e, s_=s_, slot=slot: e.matmul(
                        psall[:, pb * 512:pb * 512 + 128], lhsT=vv[:, slot, h * 128:(h + 1) * 128],
                        rhs=ets[par][:, s_ * 128:(s_ + 1) * 128], start=(s_ == 0), stop=(s_ == ns - 1)),
                        reads=[T_v[slot], T_et[par]], writes=[PSB[pb]])
                for s_ in range(ns):
                    P.op("pe", lambda e, s_=s_: e.matmul(
                        psall[:, pb * 512 + 128:pb * 512 + 256], lhsT=ones_b[:, :],
                        rhs=ets[par][:, s_ * 128:(s_ + 1) * 128], start=(s_ == 0), stop=(s_ == ns - 1)),
                        reads=[T_ones, T_et[par]], writes=[PSB[pb]])

            def att_post(it):
                e_, h, var, offs, par = it
                pb = 4 + par
                P.op("dve", lambda e: e.reciprocal(out=rrs[par][:, :], in_=psall[:, pb * 512 + 128:pb * 512 + 256]),
                     reads=[PSB[pb]], writes=[T_rr[par]])
                tcol = (e_ - e0) * 128
                P.op("dve", lambda e: e.tensor_tensor(out=attnT[:, h, tcol:tcol + 128], in0=psall[:, pb * 512:pb * 512 + 128],
                                                      in1=rrs[par][:, :], op=ALU.mult),
                     reads=[PSB[pb], T_rr[par]], writes=[T_attn])

            def ln_stat(ct):
                y = ycv[:, ct, :]
                ti = ct % 2
                P.op("act", lambda e: e.activation(out=tmpf[ti][:, :], in_=y, func=AF.Square),
                     reads=[T_ycv[ct]], writes=[T_tmp[ti]])
                P.op("pe", lambda e: e.matmul(psf(6), lhsT=ones_f[:, :], rhs=y, start=(ct == 0), stop=(ct == 7)),
                     reads=[T_ones, T_ycv[ct]], writes=[PSB[6]])
                P.op("pe", lambda e: e.matmul(psf(7), lhsT=ones_f[:, :], rhs=tmpf[ti][:, :], start=(ct == 0), stop=(ct == 7)),
                     reads=[T_ones, T_tmp[ti]], writes=[PSB[7]])

            def ln_part():
                lnm, lnr = sigt[0], sigt[1]
                T_lnl = [T_sig[0], T_sig[1]]
                P.op("dve", lambda e: e.tensor_scalar(out=lnm[:, :], in0=psf(6), scalar1=1.0 / 1024, scalar2=None, op0=ALU.mult),
                     reads=[PSB[6]], writes=T_lnl)
                P.op("dve", lambda e: e.tensor_tensor(out=lnr[:, :], in0=lnm[:, :], in1=lnm[:, :], op=ALU.mult), reads=T_lnl, writes=T_lnl)
                P.op("dve", lambda e: e.scalar_tensor_tensor(out=lnr[:, :], in0=psf(7), scalar=1.0 / 1024, in1=lnr[:, :], op0=ALU.mult, op1=ALU.subtract),
                     reads=[PSB[7]] + T_lnl, writes=T_lnl)
                rstd_chain(P, lnr[:, :], lnr[:, :], tmpf[0][:, :], [T_sig[0], T_sig[1], T_tmp[0]], 1.0)
                for ct in range(8):
                    y = ycv[:, ct, :]
                    P.op("dve", lambda e, y=y: e.tensor_tensor(out=y, in0=y, in1=lnm[:, :], op=ALU.subtract), reads=T_lnl, writes=[T_ycv[ct]])
                    P.op("dve", lambda e, y=y: e.tensor_tensor(out=y, in0=y, in1=lnr[:, :], op=ALU.mult), reads=T_lnl, writes=[T_ycv[ct]])
                    P.op("act", lambda e, y=y, ct=ct: e.activation(out=cT[:, ct, :], in_=y, func=AF.Silu,
                                                                  scale=cvec[:, CV_LNG + ct:CV_LNG + ct + 1], bias=cvec[:, CV_LNB + ct:CV_LNB + ct + 1]),
                         reads=[T_ycv[ct], T_cv], writes=[T_cT])
                if g + 1 < NGRP:
                    P.op("pool", lambda e: e.tensor_copy(out=zT[:, :, 0:512], in_=zT[:, :, 512:1024]), reads=[T_z], writes=[T_z])

            att_S(items[0])
            for i, it in enumerate(items):
                if i + 1 < len(items):
                    att_S(items[i + 1])
                att_mid(it)
                att_PV(it)
                att_post(it)
                if 8 <= i < 16:
                    ln_stat(i - 8)
                if i == 16:
                    ln_part()

        def phase_C(g):
            par = (g - 1) % 2
            if dbg and dbg_stop[0] == "E":
                if g == 1:
                    for k in range(5):
                        cast2(k)
            elif 1 <= g <= 5:
                cast2(g - 1)
            if g + 1 < NGRP:
                stage_x(g + 1, [0, 1])
            for m in range(16):
                b0 = m * 48
                bya = proj_lhs(OFF_C, b0, 8, lambda kc: attnT[:, kc, :], [T_attn])
                bga = proj_lhs(OFF_C, b0 + 8, 16, lambda kc: uT[:, kc, 0:512], [T_uT])
                P.op("act", lambda e, bga=bga: e.activation(out=sigt[0][:, :], in_=psf(bga), func=AF.Sigmoid), reads=[PSB[bga]], writes=[T_sig[0]])
                P.op("dve", lambda e, bya=bya: e.tensor_tensor(out=tmpf[0][:, :], in0=psf(bya), in1=sigt[0][:, :], op=ALU.mult),
                     reads=[PSB[bya], T_sig[0]], writes=[T_tmp[0]])
                byc = proj_lhs(OFF_C, b0 + 24, 8, lambda kc: cT[:, kc, :], [T_cT])
                bgb = proj_lhs(OFF_C, b0 + 32, 16, lambda kc: uT[:, kc, 0:512], [T_uT])
                P.op("act", lambda e, bgb=bgb: e.activation(out=sigt[1][:, :], in_=psf(bgb), func=AF.Sigmoid), reads=[PSB[bgb]], writes=[T_sig[1]])
                P.op("dve", lambda e, byc=byc: e.tensor_tensor(out=tmpf[1][:, :], in0=psf(byc), in1=sigt[1][:, :], op=ALU.mult),
                     reads=[PSB[byc], T_sig[1]], writes=[T_tmp[1]])
                P.op("pool", lambda e, m=m: e.tensor_tensor(out=mT(m, par), in0=tmpf[0][:, :], in1=tmpf[1][:, :], op=ALU.add),
                     reads=[T_tmp[0], T_tmp[1]], writes=[T_mTp[par]])

        def phase_D(g):
            e0 = 4 * g - 2
            par = (g - 1) % 2
            if g + 1 < NGRP:
                P.op("act", lambda e: e.activation(out=uT[:, :, 0:256], in_=uT[:, :, 512:768], func=AF.Copy), reads=[T_uT], writes=[T_uT])
                for t in (2, 3):
                    early_loads[(g + 1, t)] = load_x(x_ext[4 * (g + 1) + t])
            for nb in range(4):
                for t in range(4):
                    b = 4 + (cnt["pv"] % 2)
                    cnt["pv"] += 1
                    for kc in range(16):
                        w, wt = SS.rhs_block(OFF_O, nb * 16 + kc)
                        P.op("pe", lambda e, b=b, w=w, kc=kc, t=t: e.matmul(psf(b), lhsT=mT(kc, par)[:, t * 128:(t + 1) * 128], rhs=w,
                                                                          start=(kc == 0), stop=(kc == 15)),
                             reads=[wt, T_mTp[par]], writes=[PSB[b]])
                    P.op("act", lambda e, b=b, t=t, nb=nb: e.activation(out=sqj1[:, :], in_=psf(b), func=AF.Square,
                                                                       accum_out=stat[:, t * 8 + 2 + nb:t * 8 + 3 + nb]),
                         reads=[PSB[b]], writes=[T_sqj1, T_stat[t]])
                    P.op("dve", lambda e, b=b, t=t, nb=nb: e.tensor_copy(out=o1[:, t, nb * 512:(nb + 1) * 512], in_=psf(b)),
                         reads=[PSB[b]], writes=[T_o1[t]])
            ssums = [stat[:, t * 8 + 6:t * 8 + 7] for t in range(4)]
            srss = [stat[:, t * 8 + 7:t * 8 + 8] for t in range(4)]
            for t in range(4):
                P.op("dve", lambda e, t=t: e.tensor_scalar(out=sjunk[:, 0:4], in0=stat[:, t * 8 + 2:t * 8 + 6], scalar1=1.0, scalar2=0.0,
                                                          op0=ALU.mult, op1=ALU.add, accum_out=ssums[t]),
                     reads=[T_stat[t]], writes=[T_stat[t]], late=True)
            for t in range(4):
                P.op("dve", lambda e, t=t: e.tensor_scalar(out=srss[t], in0=ssums[t], scalar1=1.0 / D, scalar2=EPS, op0=ALU.mult, op1=ALU.add),
                     reads=[T_stat[t]], writes=[T_stat[t]])
            for t in range(4):
                P.op("act", lambda e, t=t: e.activation(out=sjunk[:, 4 + t:5 + t], in_=srss[t], func=AF.Sqrt), reads=[T_stat[t]], writes=[T_stat[t]])
            for t in range(4):
                P.op("dve", lambda e, t=t: e.reciprocal(out=srss[t], in_=sjunk[:, 4 + t:5 + t]), reads=[T_stat[t]], writes=[T_stat[t]], late=True)
            for t in range(4):
                ptile = e0 + t - 2
                P.op("dve", lambda e, t=t: e.scalar_tensor_tensor(out=o1[:, t, :], in0=o1[:, t, :], scalar=srss[t], in1=gpost[:, :],
                                                                 op0=ALU.mult, op1=ALU.mult),
                     reads=[T_stat[t], T_gp], writes=[T_o1[t]])
                P.op("pool", lambda e, t=t: e.dma_start(out=o1[:, t, :], in_=x_ext[e0 + t], accum_op=ALU.add),
                     reads=[T_o1[t]], writes=[T_o1[t]], dma=D_acc[t])
                P.op("pool", lambda e, t=t, ptile=ptile: e.dma_start(out=hbuf[ptile], in_=o1[:, t, :]),
                     reads=[T_o1[t]], writes=[T_hbuf[ptile]], dma=D_h[t])

        def table_prologue():
            stage_tiles = T_o1 + T_ycv + [T_attn, T_cT]
            for v_ in range(5):
                P.op("sp", lambda e, v_=v_: e.dma_start(out=ar1[:, 0:6144], in_=tabs_d[v_]), writes=stage_tiles, dma=D_tb)
                P.op("act", lambda e: e.activation(out=ar2[:, 0:6144], in_=ar1[:, 0:6144], func=AF.Exp),
                     reads=stage_tiles, writes=[T_tb])
                P.op("sp", lambda e, v_=v_: e.dma_start(out=ebt[v_], in_=ar2[:, 0:6144]), reads=[T_tb], writes=[T_ebt], dma=D_ebt)

        stop = dbg_stop[0] if dbg else None
        table_prologue()
        diag_prologue()
        for g in range(NGRP):
            phase_A(g)
            if dbg and stop == "A" and g == 1:
                dump("dbg_q", qT[:, :, :], T_q)
                dump("dbg_k", kT[:, :, :], T_k)
                dump("dbg_v", vv[:, :, :], T_v)
                dump("dbg_z", zT[:, :, :], [T_z])
                break
            if g >= 1:
                phase_B(g)
                if dbg and stop == "B" and g == 1:
                    dump("dbg_attn", ar1_bf[:, 8192:12288], [T_attn])
                    dump("dbg_c", ar1_bf[:, 12288:16384], [T_cT])
                    break
                phase_C(g)
                if dbg and stop == "C" and g == 1:
                    pass
                    break
                phase_D(g)
                if dbg and stop == "D" and g == 1:
                    dump("dbg_h", ar1[:, :], T_o1)
                    dump("dbg_stat", stat[:, :], T_stat)
                    break
                if dbg and stop == "E" and g == dbg_stop[1]:
                    break
        P.emit()

    if dbg and stop in ("A", "B", "C", "D"):
        return nc

    with ExitStack() as es:
        def sb(name, shape, dt):
            return es.enter_context(nc.sbuf_tensor(name, shape, dt))

        def sem(name):
            return es.enter_context(nc.semaphore(name))

        esems = {e: sem("f_" + e) for e in ENGS}
        P = Prog(nc, esems)
        psall = es.enter_context(nc.psum_tensor("psall2", [128, 4096], F32))
        psall_bf = psall.bitcast(BF16)
        PSB = [Tile("psb%d" % i, exclusive=True) for i in range(8)]

        def psf(b, n=512):
            return psall[:, b * 512:b * 512 + n]

        ident = sb("ident2", [128, 128], BF16)
        cvec = sb("cvec2", [128, NCV], F32)
        gpost = sb("gpost2", [128, D], F32)
        uT = sb("u2T", [128, 16, 512], BF16)
        fT = sb("fT", [128, NJ, 512], BF16)
        hh = [sb("hh%d" % i, [128, 4, D], F32) for i in range(2)]
        o2 = sb("o2", [128, 4, D], F32)
        slabs = [sb("slabf%d" % i, [128, 4096], BF16) for i in range(NSB)]
        utms = [sb("utm2_%d" % i, [128, D], BF16) for i in range(2)]
        sqj = sb("sqj", [128, 512], BF16)
        sigt = [sb("sigf%d" % i, [128, 512], F32) for i in range(2)]
        stat = sb("stat2", [128, 64], F32)
        sjunk = sb("sjunk2", [128, 8], F32)

        T_const = Tile("const")
        T_cv = Tile("cvec")
        T_gp = Tile("gpost")
        T_uT = Tile("uT")
        T_fT = [Tile("fT%d" % j) for j in range(NJ)]
        T_h = [[Tile("h%d_%d" % (j, i)) for i in range(4)] for j in range(2)]
        T_o2 = [Tile("o2_%d" % i) for i in range(4)]
        T_slab = [Tile("slab%d" % i) for i in range(NSB)]
        T_utms = [Tile("utm%d" % i) for i in range(2)]
        T_sqj = Tile("sqj")
        T_sig = [Tile("sig%d" % i) for i in range(2)]
        T_stat = [Tile("stat%d" % i) for i in range(8)]
        T_w2 = Tile("wbf_s2")
        D_slab = [P.new_dma_sem(sem("g_slab%d" % i)) for i in range(NSB)]
        D_hl = [P.new_dma_sem(sem("g_h%d" % i)) for i in range(4)]
        D_o = [P.new_dma_sem(sem("g_o%d" % i)) for i in range(4)]
        D_consts = [P.new_dma_sem(sem("g_const%d" % i)) for i in range(3)]

        P.op("pool", lambda e: e.dma_start(out=ident[:, :], in_=ident_d[:, :]), writes=[T_const], dma=D_consts[0])
        P.op("pool", lambda e: e.dma_start(out=cvec[:, :], in_=cvec_d[:, :]), writes=[T_cv], dma=D_consts[1])
        P.op("pool", lambda e: e.dma_start(out=gpost[:, :], in_=gpost_d[1]), writes=[T_gp], dma=D_consts[2])
        SS = SlabStream(P, wbf, slabs, T_slab, D_slab, {})
        cnt = {"pt": 0, "ps": 0, "sig": 0}

        NBLK2 = dbg_stop[1] if (dbg and dbg_stop[0] == "E") else 8

        def load_h(blk):
            hb = blk % 2
            for t in range(4):
                P.op("act", lambda e, t=t: e.dma_start(out=hh[hb][:, t, :], in_=hbuf[4 * blk + t]), writes=[T_h[hb][t]], dma=D_hl[t])

        def norm_T(blk):
            hb = blk % 2
            for t in range(4):
                sq = stat[:, 32 + t * 4:32 + t * 4 + 1]
                rs_ = stat[:, 32 + t * 4 + 1:32 + t * 4 + 2]
                tmp_ = stat[:, 32 + t * 4 + 2:32 + t * 4 + 3]
                ui = t % 2
                utm = utms[ui]
                T_utm = T_utms[ui]
                P.op("act", lambda e, t=t, sq=sq, utm=utm: e.activation(out=utm[:, :], in_=hh[hb][:, t, :], func=AF.Square, accum_out=sq),
                     reads=[T_h[hb][t]], writes=[T_utm, T_stat[4 + t]])
                rstd_chain(P, sq, rs_, tmp_, [T_stat[4 + t]], 1.0 / D)
                P.op("dve", lambda e, t=t, rs_=rs_, utm=utm: e.tensor_scalar(out=utm[:, :], in0=hh[hb][:, t, :], scalar1=rs_, scalar2=None, op0=ALU.mult),
                     reads=[T_h[hb][t], T_stat[4 + t]], writes=[T_utm])
                for half in range(2):
                    b = 6 + half
                    for j in range(8):
                        kc = half * 8 + j
                        P.op("pe", lambda e, b=b, j=j, kc=kc, utm=utm: e.transpose(
                            out=psall_bf[:, b * 1024 + j * 128:b * 1024 + (j + 1) * 128],
                            in_=utm[:, kc * 128:(kc + 1) * 128], identity=ident[:, :]),
                            reads=[T_utm, T_const], writes=[PSB[b]])
                    for j in range(8):
                        kc = half * 8 + j
                        src = psall_bf[:, b * 1024 + j * 128:b * 1024 + (j + 1) * 128]
                        dst = uT[:, kc, t * 128:(t + 1) * 128]
                        gs = cvec[:, CV_GPRE2 + kc:CV_GPRE2 + kc + 1]
                        if half == 0:
                            P.op("act", lambda e, src=src, dst=dst, gs=gs: e.activation(out=dst, in_=src, func=AF.Copy, scale=gs),
                                 reads=[PSB[b], T_cv], writes=[T_uT])
                        else:
                            P.op("dve", lambda e, src=src, dst=dst, gs=gs: e.tensor_scalar(out=dst, in0=src, scalar1=gs, scalar2=None, op0=ALU.mult),
                                 reads=[PSB[b], T_cv], writes=[T_uT])

        load_h(0)
        norm_T(0)
        for blk in range(NBLK2):
            hb = blk % 2
            for j in range(NJ):
                bs = []
                for gu in range(2):
                    b = cnt["ps"] % 6
                    cnt["ps"] += 1
                    for kc in range(16):
                        w, wt = SS.lhs_block(OFF_GU, (2 * j + gu) * 16 + kc)
                        P.op("pe", lambda e, b=b, w=w, kc=kc: e.matmul(psf(b), lhsT=w, rhs=uT[:, kc, :], start=(kc == 0), stop=(kc == 15)),
                             reads=[wt, T_uT], writes=[PSB[b]])
                    bs.append(b)
                si = cnt["sig"] % 2
                cnt["sig"] += 1
                P.op("act", lambda e, b=bs[0], si=si: e.activation(out=sigt[si][:, :], in_=psf(b), func=AF.Silu), reads=[PSB[bs[0]]], writes=[T_sig[si]])
                P.op("dve", lambda e, b=bs[1], si=si, j=j: e.tensor_tensor(out=fT[:, j, :], in0=psf(b), in1=sigt[si][:, :], op=ALU.mult),
                     reads=[PSB[bs[1]], T_sig[si]], writes=[T_fT[j]])
            if blk + 1 < NBLK2:
                load_h(blk + 1)
            for nq in range(4):
                banks = [(nq * 4 + t) % 6 for t in range(4)]
                for kc in range(NJ):
                    w, wt = SS.rhs_block(OFF_D, nq * NJ + kc)
                    for t in range(4):
                        b = banks[t]
                        P.op("pe", lambda e, b=b, w=w, kc=kc, t=t: e.matmul(psf(b), lhsT=fT[:, kc, t * 128:(t + 1) * 128], rhs=w,
                                                                          start=(kc == 0), stop=(kc == NJ - 1)),
                             reads=[wt, T_fT[kc]], writes=[PSB[b]])
                for t in range(4):
                    b = banks[t]
                    P.op("act", lambda e, b=b, t=t, nq=nq: e.activation(out=sqj[:, :], in_=psf(b), func=AF.Square,
                                                                       accum_out=stat[:, t * 8 + 2 + nq:t * 8 + 3 + nq]),
                         reads=[PSB[b]], writes=[T_sqj, T_stat[t]])
                    P.op("dve", lambda e, b=b, t=t, nq=nq: e.tensor_copy(out=o2[:, t, nq * 512:(nq + 1) * 512], in_=psf(b)),
                         reads=[PSB[b]], writes=[T_o2[t]])
                if nq == 1 and blk + 1 < NBLK2:
                    norm_T(blk + 1)
            ssums = [stat[:, t * 8 + 6:t * 8 + 7] for t in range(4)]
            srss = [stat[:, t * 8 + 7:t * 8 + 8] for t in range(4)]
            for t in range(4):
                P.op("dve", lambda e, t=t: e.tensor_scalar(out=sjunk[:, 0:4], in0=stat[:, t * 8 + 2:t * 8 + 6], scalar1=1.0, scalar2=0.0,
                                                          op0=ALU.mult, op1=ALU.add, accum_out=ssums[t]),
                     reads=[T_stat[t]], writes=[T_stat[t]], late=True)
            for t in range(4):
                P.op("dve", lambda e, t=t: e.tensor_scalar(out=srss[t], in0=ssums[t], scalar1=1.0 / D, scalar2=EPS, op0=ALU.mult, op1=ALU.add),
                     reads=[T_stat[t]], writes=[T_stat[t]])
            for t in range(4):
                P.op("act", lambda e, t=t: e.activation(out=sjunk[:, 4 + t:5 + t], in_=srss[t], func=AF.Sqrt), reads=[T_stat[t]], writes=[T_stat[t]])
            for t in range(4):
                P.op("dve", lambda e, t=t: e.reciprocal(out=srss[t], in_=sjunk[:, 4 + t:5 + t]), reads=[T_stat[t]], writes=[T_stat[t]], late=True)
            for t in range(4):
                P.op("dve", lambda e, t=t: e.scalar_tensor_tensor(out=o2[:, t, :], in0=o2[:, t, :], scalar=srss[t], in1=gpost[:, :],
                                                                 op0=ALU.mult, op1=ALU.mult),
                     reads=[T_stat[t], T_gp], writes=[T_o2[t]])
                P.op("dve", lambda e, t=t, hb=hb: e.tensor_tensor(out=o2[:, t, :], in0=o2[:, t, :], in1=hh[hb][:, t, :], op=ALU.add),
                     reads=[T_h[hb][t]], writes=[T_o2[t]])
                P.op("act", lambda e, t=t, blk=blk: e.dma_start(out=out_d[4 * blk + t], in_=o2[:, t, :]),
                     reads=[T_o2[t]], writes=[], dma=D_o[t])
        P.emit()
    return nc


dbg_stop = [None, 1]


def _lhs_blocks(W, cols_tiles):
    K = W.shape[0]
    kc = K // 128
    out = np.empty((len(cols_tiles) * kc, 128, 128), np.float32)
    i = 0
    for c in cols_tiles:
        blk = W[:, c:c + 128].reshape(kc, 128, 128)
        out[i:i + kc] = blk
        i += kc
    return out


def _pack_lhs(blocks):
    nb = blocks.shape[0]
    assert nb % 32 == 0
    return np.ascontiguousarray(blocks.reshape(nb // 32, 32, 128, 128).transpose(0, 2, 1, 3)).reshape(nb // 32 * 128, 4096)


def _rhs_blocks(W, col_starts):
    K = W.shape[0]
    kc = K // 128
    out = np.empty((len(col_starts) * kc, 128, 512), np.float32)
    i = 0
    for c in col_starts:
        out[i:i + kc] = W[:, c:c + 512].reshape(kc, 128, 512)
        i += kc
    return out


def _pack_rhs(blocks):
    nb = blocks.shape[0]
    assert nb % 8 == 0
    return np.ascontiguousarray(blocks.reshape(nb // 8, 8, 128, 512).transpose(0, 2, 1, 3)).reshape(nb // 8 * 128, 4096)


def _build_wall(w_in, w_attn_o, w_conv_o, w_out, w_gate_up, w_down):
    parts = []
    cols = []
    for j in range(8):
        cols += [3072 + 128 * j, 4096 + 128 * j]
    cols += [128 * m for m in range(8)] + [1024 + 128 * m for m in range(8)]
    parts.append(_pack_lhs(_lhs_blocks(w_in, cols)))
    parts.append(_pack_rhs(_rhs_blocks(w_in, [2048, 2560])))
    blks = []
    for m in range(16):
        blks.append(_lhs_blocks(w_attn_o, [128 * m]))
        blks.append(_lhs_blocks(w_in, [5120 + 128 * m]))
        blks.append(_lhs_blocks(w_conv_o, [128 * m]))
        blks.append(_lhs_blocks(w_in, [7168 + 128 * m]))
    parts.append(_pack_lhs(np.concatenate(blks, 0)))
    parts.append(_pack_rhs(_rhs_blocks(w_out, [0, 512, 1024, 1536])))
    cols = []
    for j in range(NJ):
        cols += [128 * j, DFF + 128 * j]
    parts.append(_pack_lhs(_lhs_blocks(w_gate_up, cols)))
    parts.append(_pack_rhs(_rhs_blocks(w_down, [0, 512, 1024, 1536])))
    wall = np.concatenate(parts, 0)
    assert wall.shape == (NSLAB * 128, 4096), wall.shape
    return wall


def _build_tables(rpb, qd):
    NEG = np.float32(-30000.0)
    tabs = np.full((5, 128, 8, 6, 128), NEG, np.float32)
    kk = np.arange(128)
    kr2, kcol = kk // 64, kk % 64
    qq = np.arange(128)
    qr2, qcol = qq // 64, qq % 64
    for vi, p in enumerate([0, 1, 2, 30, 31]):
        offs = {0: [-2, -1, 0, 1, 2, 3], 4: [-3, -2, -1, 0, 1, 2]}.get(vi, [-2, -1, 0, 1, 2])
        r0 = 64 * qd + 2 * p
        for s, o in enumerate(offs):
            krow = (r0 + 2 * o + kr2)[:, None]
            qrow = (r0 + qr2)[None, :]
            rs = np.clip(qrow - 4, 0, 248)
            cs = np.clip(qcol - 8, 0, 48)[None, :]
            kc_ = kcol[:, None]
            ok = (krow >= rs) & (krow < rs + 8) & (kc_ >= cs) & (kc_ < cs + 16) & (krow >= 0) & (krow < 256)
            dr = np.clip(krow - qrow + 7, 0, 14)
            dc = np.clip(kc_ - qcol[None, :] + 15, 0, 30)
            for h in range(8):
                vals = rpb[h][dr, dc]
                tabs[vi, :, h, s, :] = np.where(ok, vals, NEG)
    return tabs.reshape(5, 128, 6144)


def _build_cvec(mix_pre_g, ffn_pre_g, w_dw, b_dw, ln_g, ln_b):
    cv = np.zeros((128, NCV), np.float32)
    cv[:, CV_GPRE1:CV_GPRE1 + 16] = mix_pre_g.reshape(16, 128).T
    cv[:, CV_GPRE2:CV_GPRE2 + 16] = ffn_pre_g.reshape(16, 128).T
    cv[:, CV_WDW:CV_WDW + 248] = w_dw.reshape(31, 8, 128).transpose(2, 1, 0).reshape(128, 248)
    cv[:, CV_BDW:CV_BDW + 8] = b_dw.reshape(8, 128).T
    cv[:, CV_LNG:CV_LNG + 8] = ln_g.reshape(8, 128).T
    cv[:, CV_LNB:CV_LNB + 8] = ln_b.reshape(8, 128).T
    return cv


def _prep_inputs(x, mix_pre_g, mix_post_g, w_in, rpb, w_attn_o, w_dw, b_dw, conv_ln_g, conv_ln_b,
                 w_conv_o, w_out, ffn_pre_g, ffn_post_g, w_gate_up, w_down):
    f = lambda a: np.asarray(a, np.float32)
    x = f(x)
    wall = _build_wall(f(w_in)[0], f(w_attn_o)[0], f(w_conv_o)[0], f(w_out)[0], f(w_gate_up)[0], f(w_down)[0])
    cv = _build_cvec(f(mix_pre_g)[0], f(ffn_pre_g)[0], f(w_dw)[0], f(b_dw)[0], f(conv_ln_g)[0], f(conv_ln_b)[0])
    gp = np.stack([np.broadcast_to(f(mix_post_g)[0], (128, D)), np.broadcast_to(f(ffn_post_g)[0], (128, D))]).astype(np.float32)
    gp = np.ascontiguousarray(gp)
    ident = np.eye(128, dtype=np.float32)
    tabs_q = [_build_tables(f(rpb)[0], qd) for qd in range(4)]
    in_maps = []
    for c in range(NCORES):
        b, qd = c // 4, c % 4
        xe = np.zeros((NT_EXT * 128, D), np.float32)
        lo = 4096 * qd - 256
        hi = 4096 * qd + 4096 + 256
        slo, shi = max(lo, 0), min(hi, 16384)
        xe[slo - lo:shi - lo] = x[b, slo:shi]
        in_maps.append({"x_ext": xe.reshape(NT_EXT, 128, D), "wall": wall, "cvec": cv, "gpost": gp,
                        "tabs": tabs_q[qd], "ident": ident})
    return in_maps


def kernel(**inputs):
    in_maps = _prep_inputs(**inputs)
    nc = build_program()
    res = run_bass_kernel_spmd(nc, in_maps, core_ids=list(range(NCORES)))
    out = np.empty((2, 16384, D), np.float32)
    for c in range(NCORES):
        b, qd = c // 4, c % 4
        out[b, 4096 * qd:4096 * (qd + 1)] = np.asarray(res.results[c]["out"]).reshape(4096, D)
    return out
```
